# Optimizing a Trainium2 kernel written in Bass

```python
import jax, jax.numpy as jnp
from jax import lax
import numpy as np

D_MODEL = 1024
BATCH = 2
SEQ = 8192
DEPTH = 2

PLE_DIM = 256
N_BRANCH = 4
BRANCH_WIDTH = 256
N_GROUPS = 4
GROUP_DIM = BRANCH_WIDTH // N_GROUPS
CHUNK = 128
CONF_K = 31
SHORT_K = 3
FFN_K = 3
D_FF = 2816
POOL_WINDOWS = (2, 4, 8, 16)
EPS = 1e-6

A_COLS = 2 * BRANCH_WIDTH
B_COLS = 2 * BRANCH_WIDTH
C_COLS = 3 * BRANCH_WIDTH
D_COLS = BRANCH_WIDTH
GATE_COLS = N_BRANCH * D_MODEL
IN_COLS = A_COLS + B_COLS + C_COLS + D_COLS + GATE_COLS
SPLITS = (A_COLS, A_COLS + B_COLS, A_COLS + B_COLS + C_COLS, A_COLS + B_COLS + C_COLS + D_COLS)

kernel_name = "hybrid_gated_conv_pool_gmlp_trunk"


def rms_norm(x, g):
    xf = x.astype(jnp.float32)
    y = xf * lax.rsqrt(jnp.mean(xf * xf, axis=-1, keepdims=True) + EPS)
    return (y * g.astype(jnp.float32)).astype(x.dtype)


def layer_norm(x, g, b):
    xf = x.astype(jnp.float32)
    mu = jnp.mean(xf, axis=-1, keepdims=True)
    xc = xf - mu
    var = jnp.mean(xc * xc, axis=-1, keepdims=True)
    y = xc * lax.rsqrt(var + EPS)
    return (y * g.astype(jnp.float32) + b.astype(jnp.float32)).astype(x.dtype)


def causal_dwconv(x, w):
    K, C = w.shape
    return lax.conv_general_dilated(
        x, w[:, None, :].astype(x.dtype), window_strides=(1,), padding=[(K - 1, 0)],
        dimension_numbers=('NWC', 'WIO', 'NWC'), feature_group_count=C)


def gmlp_spatial_gating(z, ln_g, ln_b, w_s, b_s):
    Bn, S, _ = z.shape
    z = jax.nn.gelu(z)
    u, v = jnp.split(z, 2, axis=-1)
    v = layer_norm(v, ln_g, ln_b)
    v = v.reshape(Bn, S // CHUNK, CHUNK, N_GROUPS, GROUP_DIM)
    causal = jnp.tril(jnp.ones((CHUNK, CHUNK), dtype=bool))
    w = jnp.where(causal[None], w_s, 0)
    mixed = jnp.einsum('gts,bnsgc->bntgc', w, v) + b_s.T[None, None, :, :, None]
    return u * mixed.reshape(Bn, S, BRANCH_WIDTH)


def conformer_conv(z, w_dw, b_dw, ln_g, ln_b):
    Bn, S, _ = z.shape
    a, g = jnp.split(z, 2, axis=-1)
    y = causal_dwconv(a * jax.nn.sigmoid(g), w_dw) + b_dw
    y = layer_norm(y.reshape(Bn, S, N_GROUPS, GROUP_DIM), ln_g, ln_b)
    return jax.nn.silu(y).reshape(Bn, S, BRANCH_WIDTH)


def short_gated_conv(z, w):
    b_gate, c_gate, xin = jnp.split(z, 3, axis=-1)
    return b_gate * causal_dwconv(c_gate * xin, w)


def multiscale_pool(z, w_grp, scale):
    Bn, S, _ = z.shape
    zf = z.reshape(Bn, S, N_GROUPS, GROUP_DIM).astype(jnp.float32)
    cs = jnp.cumsum(zf, axis=1)
    t = jnp.arange(S)
    outs = []
    for gi, win in enumerate(POOL_WINDOWS):
        c = cs[:, :, gi]
        prev = jnp.pad(c, ((0, 0), (win, 0), (0, 0)))[:, :S]
        cnt = jnp.minimum(t + 1, win).astype(jnp.float32)[None, :, None]
        outs.append((c - prev) / cnt - zf[:, :, gi])
    pooled = jnp.stack(outs, axis=2).astype(z.dtype)
    y = jnp.einsum('bsgc,gcd->bsgd', pooled, w_grp) * scale
    return y.reshape(Bn, S, BRANCH_WIDTH)


def conv_glu_ffn(h, w_up, w_conv, b_conv, w_down):
    up = causal_dwconv(h @ w_up, w_conv) + b_conv
    gate, val = jnp.split(up, 2, axis=-1)
    return (jax.nn.silu(gate) * val) @ w_down


def setup_inputs(seed: int = 0) -> dict:
    key = jax.random.key(seed)
    ks = jax.random.split(key, 32)
    nrm = lambda k, shape, s: jax.random.normal(k, shape, jnp.float32) * s
    L, D, BW, G, GD = DEPTH, D_MODEL, BRANCH_WIDTH, N_GROUPS, GROUP_DIM
    return {
        "x": nrm(ks[0], (BATCH, SEQ, D), 1.0),
        "p": nrm(ks[1], (L, BATCH, SEQ, PLE_DIM), 1.0),
        "g_mix": 1.0 + nrm(ks[2], (L, D), 0.02),
        "w_in": nrm(ks[3], (L, D, IN_COLS), D ** -0.5),
        "gmlp_ln_g": 1.0 + nrm(ks[4], (L, BW), 0.02),
        "gmlp_ln_b": nrm(ks[5], (L, BW), 0.02),
        "gmlp_w_s": nrm(ks[6], (L, G, CHUNK, CHUNK), CHUNK ** -0.5),
        "gmlp_b_s": 1.0 + nrm(ks[7], (L, G, CHUNK), 0.02),
        "conf_w_dw": nrm(ks[8], (L, CONF_K, BW), CONF_K ** -0.5),
        "conf_b_dw": nrm(ks[9], (L, BW), 0.02),
        "conf_ln_g": 1.0 + nrm(ks[10], (L, G, GD), 0.02),
        "conf_ln_b": nrm(ks[11], (L, G, GD), 0.02),
        "short_w": nrm(ks[12], (L, SHORT_K, BW), SHORT_K ** -0.5),
        "pool_w": nrm(ks[13], (L, G, GD, GD), GD ** -0.5),
        "pool_scale": 1.0 + nrm(ks[14], (L, G, GD), 0.02),
        "w_branch": nrm(ks[15], (L, N_BRANCH, BW, D), BW ** -0.5),
        "w_out": nrm(ks[16], (L, D, D), D ** -0.5),
        "g_ffn": 1.0 + nrm(ks[17], (L, D), 0.02),
        "ffn_w_up": nrm(ks[18], (L, D, 2 * D_FF), D ** -0.5),
        "ffn_w_conv": nrm(ks[19], (L, FFN_K, 2 * D_FF), FFN_K ** -0.5),
        "ffn_b_conv": nrm(ks[20], (L, 2 * D_FF), 0.02),
        "ffn_w_down": nrm(ks[21], (L, D_FF, D), D_FF ** -0.5),
        "g_ple": 1.0 + nrm(ks[22], (L, D), 0.02),
        "ple_w_gate": nrm(ks[23], (L, D, D), D ** -0.5),
        "ple_w_proj": nrm(ks[24], (L, PLE_DIM, D), PLE_DIM ** -0.5),
        "g_final": 1.0 + nrm(ks[25], (D,), 0.02),
    }


def reference(x, p, g_mix, w_in, gmlp_ln_g, gmlp_ln_b, gmlp_w_s, gmlp_b_s,
              conf_w_dw, conf_b_dw, conf_ln_g, conf_ln_b, short_w, pool_w, pool_scale,
              w_branch, w_out, g_ffn, ffn_w_up, ffn_w_conv, ffn_b_conv, ffn_w_down,
              g_ple, ple_w_gate, ple_w_proj, g_final):
    Bn, S, _ = x.shape
    for i in range(DEPTH):
        h = rms_norm(x, g_mix[i])
        proj = h @ w_in[i]
        za, zb, zc, zd, zg = jnp.split(proj, SPLITS, axis=-1)
        ya = gmlp_spatial_gating(za, gmlp_ln_g[i], gmlp_ln_b[i], gmlp_w_s[i], gmlp_b_s[i])
        yb = conformer_conv(zb, conf_w_dw[i], conf_b_dw[i], conf_ln_g[i], conf_ln_b[i])
        yc = short_gated_conv(zc, short_w[i])
        yd = multiscale_pool(zd, pool_w[i], pool_scale[i])
        branches = jnp.stack([ya, yb, yc, yd], axis=2)
        br = jnp.einsum('bskc,kcd->bskd', branches, w_branch[i])
        gates = jax.nn.sigmoid(zg.reshape(Bn, S, N_BRANCH, D_MODEL))
        merged = jnp.einsum('bskd,bskd->bsd', br, gates)
        x = x + merged @ w_out[i]
        h = rms_norm(x, g_ffn[i])
        x = x + conv_glu_ffn(h, ffn_w_up[i], ffn_w_conv[i], ffn_b_conv[i], ffn_w_down[i])
        h = rms_norm(x, g_ple[i])
        x = x + jax.nn.sigmoid(h @ ple_w_gate[i]) * (p[i] @ ple_w_proj[i])
    return rms_norm(x, g_final)
```

```python
from contextlib import ExitStack
import numpy as np
import concourse.bass as bass
import concourse.mybir as mybir
from concourse.bass_utils import run_bass_kernel_spmd

F32 = mybir.dt.float32
BF16 = mybir.dt.bfloat16
AF = mybir.ActivationFunctionType
ALU = mybir.AluOpType
AX = mybir.AxisListType

L = 2
D = 1024
KD = 8
HALO = 256
TOUT = 1024
T = HALO + TOUT
PAD = 32
NPASS = 2
SUBT = [(0, 512), (512, 512), (1024, 256)]
DFF = 2816
NJ = 22
JGROUPS = [(0, 8), (8, 8), (16, 6)]
EPS = 1e-6
WINS = (2, 4, 8, 16)


class _Op:
    __slots__ = ('eng', 'emit', 'is_dma', 'dma_key', 'deps', 'signal', 'token', 'waits', 'idx', 'snapshot', 'stage', 'hoist')


class Sched:
    def __init__(self):
        self.ops = []
        self.last_writer = {}
        self.readers = {}
        self.stage = ''

    def add(self, eng, emit, reads=(), writes=(), dma_key=None, hoist=False):
        op = _Op()
        op.hoist = hoist
        op.eng = eng
        op.emit = emit
        op.is_dma = dma_key is not None
        op.dma_key = dma_key
        op.signal = op.is_dma
        op.token = None
        op.idx = len(self.ops)
        op.stage = self.stage
        deps = {}
        for k in reads:
            w = self.last_writer.get(k)
            if w is not None:
                deps[w.idx] = (w, True)
        for k in writes:
            w = self.last_writer.get(k)
            if w is not None:
                deps[w.idx] = (w, True)
            for r in self.readers.get(k, ()):
                if r.idx not in deps:
                    deps[r.idx] = (r, False)
        for k in reads:
            self.readers.setdefault(k, []).append(op)
        for k in writes:
            self.last_writer[k] = op
            self.readers[k] = []
        op.deps = []
        for i in sorted(deps):
            d, hard = deps[i]
            if not d.is_dma and not op.is_dma and d.eng == op.eng:
                if op.eng == 'pe' or not hard:
                    continue
            d.signal = True
            op.deps.append(d)
        self.ops.append(op)
        return op

    def finalize(self, nc, stack):
        keyed = []
        for op in self.ops:
            if op.hoist:
                m = max([d.idx for d in op.deps], default=-1)
                keyed.append(((m, 1, op.idx), op))
            else:
                keyed.append(((op.idx, 0, 0), op))
        keyed.sort(key=lambda kv: kv[0])
        self.ops = [op for _, op in keyed]
        self.sems = {}
        cnt = {}
        for op in self.ops:
            if not op.signal:
                continue
            key = ('dma', op.dma_key) if op.is_dma else ('eng', op.eng)
            if key not in self.sems:
                self.sems[key] = stack.enter_context(nc.semaphore("s%d" % len(self.sems)))
                cnt[key] = 0
            cnt[key] += 16 if op.is_dma else 1
            op.token = (key, cnt[key])
        known = {}
        for op in self.ops:
            K = known.setdefault(op.eng, {})
            waits = {}
            for d in op.deps:
                key, val = d.token
                if K.get(key, 0) >= val:
                    continue
                waits[key] = max(waits.get(key, 0), val)
                for s, v in d.snapshot.items():
                    if K.get(s, 0) < v:
                        K[s] = v
                K[key] = val
            op.waits = list(waits.items())
            op.snapshot = dict(K)

    def emit_engine(self, name, eng, final_waits=()):
        for op in self.ops:
            if op.eng != name:
                continue
            for key, val in op.waits:
                eng.wait_ge(self.sems[key], val)
            ins = op.emit(eng)
            if op.signal:
                ins.then_inc(self.sems[op.token[0]], 16 if op.is_dma else 1)
        for key, val in final_waits:
            eng.wait_ge(self.sems[key], val)


class _Stop(Exception):
    pass


def build_nc(debug=None):
    nc = bass.Bass("TRN2", target_bir_lowering=False)

    def din(name, shape):
        return nc.dram_tensor(name, list(shape), F32, kind="ExternalInput").ap()

    xT = din("xT", [NPASS, D, T])
    pT = din("pT", [NPASS, L, 256, T])
    maskd = din("mask", [128, HALO])
    cntd = din("cnt", [128, 2, 16])
    w_in = din("w_in", [L, D, 6144])
    w_branch = din("w_branch", [L, 4, 256, D])
    w_out = din("w_out", [L, D, D])
    w_up = din("w_up", [L, D, 2 * DFF])
    w_down = din("w_down", [L, DFF, D])
    w_pg = din("w_pg", [L, D, D])
    w_pp = din("w_pp", [L, 256, D])
    gcols_d = din("gcols", [128, L * 3 * KD + KD])
    lncols_d = din("lncols", [128, L * 6 * 2])
    shortw_d = din("shortw", [128, L * 3 * 2])
    confw_d = din("confw", [128, L * 31 * 2])
    ffnwc_d = din("ffnwc", [128, L * 3 * 44])
    ffnbc_d = din("ffnbc", [128, L * 44])
    gwT_d = din("gwT", [L, 4, 128, 128])
    gbs_d = din("gbs", [1, L * 4 * 128])
    poolbd_d = din("poolbd", [L, 2, 128, 128])
    consts_d = din("consts", [6 + 20, 128, 128])
    outT = nc.dram_tensor("outT", [NPASS, D, TOUT], F32, kind="ExternalOutput").ap()

    S = Sched()
    st = ExitStack()
    with st:
        def sb(name, shape, dt=F32):
            return st.enter_context(nc.sbuf_tensor(name, list(shape), dt))

        X = sb("X", [128, KD, T])
        H = sb("H", [128, KD, T], BF16)
        Z = sb("Z", [128, 4, PAD + T], BF16)
        Y = sb("Y", [128, 8, T], BF16)
        MG = sb("MG", [128, KD, T], BF16)
        UG = sb("UG", [128, PAD + T])
        UV = sb("UV", [128, PAD + T])
        MASK = sb("MASK", [128, HALO])
        CNT = sb("CNT", [128, 2, 16])
        ZN2 = sb("ZN2", [128, 12, 128], BF16)
        BIAS2 = sb("BIAS2", [128, 2, 128])
        GWT = sb("GWT", [128, 4, 128], BF16)
        GWTs = sb("GWTs", [128, 4, 128], BF16)
        GBSb = sb("GBSb", [1, L * 4 * 128], BF16)
        POOLBD = sb("POOLBD", [128, 2, 128], BF16)
        CST = sb("CST", [128, 2, 128])
        CSTb = sb("CSTb", [128, 26, 128], BF16)
        DG31 = sb("DG31", [128, 31, 128], BF16)
        DG3 = sb("DG3", [128, 6, 128], BF16)
        GC = sb("GC", [128, L * 3 * KD + KD])
        LNC = sb("LNC", [128, L * 12])
        SHW = sb("SHW", [128, L * 6])
        CFW = sb("CFW", [128, L * 62])
        FWC = sb("FWC", [128, L * 132])
        FBC = sb("FBC", [128, L * 44])
        EPSC = sb("EPSC", [128, 1])
        NT = 8
        TMP = [sb("TMP%d" % i, [128, 512]) for i in range(NT)]
        TB = [sb("TB%d" % i, [128, 512], BF16) for i in range(4)]
        COL = sb("COL", [128, 8])
        COLS = sb("COLS", [128, 16])
        COLQ = sb("COLQ", [128, 16])
        COLM = sb("COLM", [128, 16])
        NWB = 5
        WB = [sb("WB%d" % i, [128, 2048], BF16) for i in range(NWB)]
        ACC = [sb("ACC%d" % i, [128, 512]) for i in range(len(SUBT))]
        PS = [st.enter_context(nc.psum_tensor("PS%d" % i, [128, 512], F32)) for i in range(8)]

        IDENT, TRI, SELA, SELB, GAVG, ONES = 0, 1, 2, 3, 4, 5
        state = {'ps': 0, 'ws': 0, 'wb': 0, 'tmp': 0, 'tb': 0, 'dmaq': 0}

        def ps_next():
            i = state['ps']; state['ps'] = (i + 1) % 8
            return PS[i], ('ps', i)

        def tmp_next():
            i = state['tmp']; state['tmp'] = (i + 1) % NT
            return TMP[i], ('tmp', i)

        def tb_next():
            i = state['tb']; state['tb'] = (i + 1) % 4
            return TB[i], ('tb', i)

        def dma(out_ap, in_ap, wkey, reads=()):
            S.add('sp', lambda e: e.dma_start(out=out_ap, in_=in_ap), reads=list(reads), writes=[wkey], dma_key=wkey)

        def load_w(src3, kc, ncol):
            j = state['wb']; state['wb'] = (j + 1) % NWB
            n = kc * ncol
            bview = WB[j][:, 0:n].rearrange("p (k n) -> p k n", k=kc)
            cdma(bview, src3, [('wb', j)], ('wb', j), hoist=True)
            return bview, ('wb', j)

        def cdma(out_ap, in_ap, wkeys, dkey, hoist=False):
            S.add('pool', lambda e: e.dma_start(out=out_ap, in_=in_ap), writes=list(wkeys), dma_key=dkey, hoist=hoist)

        def mm(out_ap, pairs, reads, wkey):
            def f(e):
                n = len(pairs)
                for q, (l, r) in enumerate(pairs):
                    ins = e.matmul(out_ap, lhsT=l, rhs=r, start=(q == 0), stop=(q == n - 1))
                return ins
            S.add('pe', f, reads=list(reads), writes=[wkey])

        def act(out, in_, func, reads, writes, **kw):
            S.add('act', lambda e: e.activation(out=out, in_=in_, func=func, **kw), reads=list(reads), writes=list(writes))

        def tt(eng, out, a, b, op, reads, writes):
            S.add(eng, lambda e: e.tensor_tensor(out=out, in0=a, in1=b, op=op), reads=list(reads), writes=list(writes))

        def ts(eng, out, a, s1, s2, op0, op1, reads, writes):
            if s2 is None:
                S.add(eng, lambda e: e.tensor_scalar(out=out, in0=a, scalar1=s1, scalar2=None, op0=op0), reads=list(reads), writes=list(writes))
            else:
                S.add(eng, lambda e: e.tensor_scalar(out=out, in0=a, scalar1=s1, scalar2=s2, op0=op0, op1=op1), reads=list(reads), writes=list(writes))

        def stt(eng, out, a, sc, b, op0, op1, reads, writes):
            S.add(eng, lambda e: e.scalar_tensor_tensor(out=out, in0=a, scalar=sc, in1=b, op0=op0, op1=op1), reads=list(reads), writes=list(writes))

        def cp(eng, out, in_, reads, writes):
            if eng == 'act':
                S.add(eng, lambda e: e.activation(out=out, in_=in_, func=AF.Copy), reads=list(reads), writes=list(writes))
            else:
                S.add(eng, lambda e: e.tensor_copy(out=out, in_=in_), reads=list(reads), writes=list(writes))

        def rsqrt_into(out, in_ap, scale, reads, wkey, n):
            ts('dve', out, in_ap, scale, EPS, ALU.mult, ALU.add, reads, [wkey])
            act(out, out, AF.Sqrt, [wkey], [wkey])
            S.add('dve', lambda e: e.reciprocal(out=out, in_=out), reads=[wkey], writes=[wkey])

        dma(CST[:], consts_d[0:2].rearrange("c p n -> p c n"), 'CST')
        cdma(CSTb[:], consts_d.rearrange("c p n -> p c n"), ['CSTb'], 'CSTb')
        dma(GC[:], gcols_d, 'GC'); dma(LNC[:], lncols_d, 'LNC'); dma(SHW[:], shortw_d, 'SHW')
        dma(CFW[:], confw_d, 'CFW'); dma(FWC[:], ffnwc_d, 'FWC'); dma(FBC[:], ffnbc_d, 'FBC')
        cdma(GBSb[:], gbs_d, ['GBSb'], 'GBSb')
        dma(MASK[:], maskd, 'MASK')
        dma(CNT[:], cntd, 'CNT')
        S.add('pool', lambda e: e.memset(EPSC[:], EPS), writes=['EPSC'])
        S.add('pool', lambda e: e.memset(ZN2[:], 0.0), writes=['ZN2'])
        for zi in range(4):
            S.add('pool', (lambda zi: lambda e: e.memset(Z[:, zi, 0:PAD], 0.0))(zi), writes=[('Zpad', zi)])
        S.add('pool', lambda e: e.memset(UG[:, 0:PAD], 0.0), writes=['UGpad'])
        S.add('pool', lambda e: e.memset(UV[:, 0:PAD], 0.0), writes=['UVpad'])

        def xk(k, i): return ('X', k, i)

        st0 = {'v': 0}

        def cur(i):
            o, n = SUBT[i]
            if i == 0:
                return st0['v'], n - st0['v']
            return o, n

        def rmsnorm(gbase, hk, masked, pq=0):
            for i in range(len(SUBT)):
                o, n = cur(i)
                pst, pk = ps_next()
                tbs = []
                for k in range(KD):
                    tb, tk = tb_next()
                    act(tb[:, 0:n], X[:, k, o:o + n], AF.Square, [xk(k, i)], [tk])
                    tbs.append((tb, tk))
                    if len(tbs) == 4 or k == KD - 1:
                        k0 = k - len(tbs) + 1
                        for q, (tb2, tk2) in enumerate(tbs):
                            kk = k0 + q
                            S.add('pe', (lambda tb2, kk, pst, n: lambda e: e.matmul(pst[:, 0:n], lhsT=CSTb[:, ONES, :], rhs=tb2[:, 0:n], start=(kk == 0), stop=(kk == KD - 1)))(tb2, kk, pst, n),
                                  reads=[tk2, 'CSTb'], writes=[pk])
                        tbs = []
                r, rk = tmp_next()
                rsqrt_into(r[:, 0:n], pst[:, 0:n], 1.0 / D, [pk], rk, n)
                if masked and pq == 0 and i == 0:
                    tt('dve', r[:, 0:HALO - o], r[:, 0:HALO - o], MASK[:, o:HALO], ALU.mult, [rk, 'MASK'], [rk])
                for k in range(KD):
                    stt('dve', H[:, k, o:o + n], X[:, k, o:o + n], GC[:, gbase + k:gbase + k + 1], r[:, 0:n], ALU.mult, ALU.mult,
                        [xk(k, i), rk, 'GC'], [(hk, k, i)])

        def hreads(i): return [('H', k, i) for k in range(KD)]

        def proj(wcols, wk, i, c):
            o, n = cur(i)
            pst, pk = ps_next()
            mm(pst[:, 0:n], [(wcols[:, k, c * 128:(c + 1) * 128], H[:, k, o:o + n]) for k in range(KD)], hreads(i) + [wk], pk)
            return pst, pk, n

        def conv_diag(pst, n, pk, mats, src_fn, reads):
            mm(pst[:, 0:n], [(mats[q], src_fn(q)) for q in range(len(mats))], reads, pk)

        outs = []

        def chk(tag, l, pq):
            S.stage = 'after_%s_L%d_P%d' % (tag, l, pq)
            if debug is not None and debug == (tag, l) and pq == 0:
                raise _Stop()

        dbg = {}
        if debug is not None:
            for nm, tns, dt in (("dX", X, F32), ("dH", H, BF16), ("dY", Y, BF16), ("dMG", MG, BF16), ("dZ", Z, BF16), ("dUG", UG, F32)):
                dbg[nm] = (nc.dram_tensor(nm, list(tns.shape), dt, kind="ExternalOutput").ap(), tns)
        try:
          for pq in range(NPASS):
              for k in range(KD):
                  for i, (o, n) in enumerate(SUBT):
                      dma(X[:, k, o:o + n], xT[pq, k * 128:(k + 1) * 128, o:o + n], xk(k, i))
              for l in range(L):
                  lc = l * 12
                  LNG, LNB, CBD, CLG, CLB, PSC = [lc + 2 * q for q in range(6)]
                  cdma(GWTs[:], gwT_d[l].rearrange("g s t -> s g t"), ['GWTs'], 'GWTs')
                  for g in range(4):
                      tt('dve', GWT[:, g, :], GWTs[:, g, :], CSTb[:, TRI, :], ALU.mult, ['GWTs', 'CSTb'], [('GWT', g)])
                  cdma(POOLBD[:], poolbd_d[l].rearrange("j c d -> c j d"), ['POOLBD'], 'POOLBD')
                  for q in range(3):
                      for j in range(2):
                          ts('dve', DG3[:, q * 2 + j, :], CST[:, IDENT, :], SHW[:, l * 6 + q * 2 + j:l * 6 + q * 2 + j + 1], None, ALU.mult, None,
                             ['CST', 'SHW'], [('DG3', q, j)])
                  for j in range(2):
                      p1, k1 = ps_next()
                      mm(p1[:, 0:128], [(CSTb[:, SELA, :], GWT[:, 2 * j, :]), (CSTb[:, SELB, :], GWT[:, 2 * j + 1, :])],
                         ['CSTb', ('GWT', 2 * j), ('GWT', 2 * j + 1)], k1)
                      p2, k2 = ps_next()
                      b0 = (l * 4 + 2 * j) * 128
                      mm(p2[:, 0:128], [(CSTb[0:1, SELA, :], GBSb[0:1, b0:b0 + 128]), (CSTb[0:1, SELB, :], GBSb[0:1, b0 + 128:b0 + 256])],
                         ['CSTb', 'GBSb'], k2)
                      t2, tk2 = tmp_next()
                      cp('act', t2[:, 0:128], p2[:, 0:128], [k2], [tk2])
                      stt('dve', BIAS2[:, j, :], p1[:, 0:128], LNC[:, LNB + j:LNB + j + 1], t2[:, 0:128], ALU.mult, ALU.add,
                          [k1, tk2, 'LNC'], [('BIAS2', j)])

                  st0['v'] = 0 if l == 0 else 128
                  rmsnorm((l * 3 + 0) * KD, 'H', True, pq)

                  def zkey(zi, i): return ('Z', zi, i)

                  def evac_z(pst, pk, n, zi, i):
                      o = cur(i)[0]
                      cp('act', Z[:, zi, PAD + o:PAD + o + n], pst[:, 0:n], [pk], [zkey(zi, i)])

                  def gelu_from_psum(pst, pk, n, out_ap, wkeys):
                      act(out_ap, pst[:, 0:n], AF.Gelu_apprx_tanh, [pk], wkeys)

                  chk('h', l, pq)
                  wA, wAk = load_w(w_in[l].rearrange("(k p) n -> p k n", p=128)[:, :, 0:256], KD, 256)
                  wV, wVk = load_w(w_in[l].rearrange("(k p) n -> p k n", p=128)[:, :, 256:512], KD, 256)
                  for i in range(len(SUBT)):
                      o, n = cur(i)
                      for j in range(2):
                          pst, pk, _ = proj(wA, wAk, i, j)
                          gelu_from_psum(pst, pk, n, Y[:, j, o:o + n], [('Y', j, i)])
                  def gzb(ch):
                      return Z[:, ch // 5, PAD + (ch % 5) * 256:PAD + (ch % 5) * 256 + 256]

                  def gzkeys(ch):
                      return [zkey(ch // 5, iz) for iz in range(len(SUBT))]

                  def v_front(ch):
                      t0 = ch * 128
                      i = t0 // 512
                      pst, pk = ps_next()
                      mm(pst[:, 0:256], [(H[:, k, t0:t0 + 128], wV[:, k, 0:256]) for k in range(KD)], hreads(i) + [wVk], pk)
                      gz, gk = tmp_next()
                      gelu_from_psum(pst, pk, 256, gz[:, 0:256], [gk])
                      sq, sk = tmp_next()
                      act(sq[:, 0:256], gz[:, 0:256], AF.Square, [gk], [sk])
                      S.add('dve', (lambda gz, ch: lambda e: e.reduce_sum(out=COLS[:, ch:ch + 1], in_=gz[:, 0:256], axis=AX.X))(gz, ch), reads=[gk], writes=[('COLS', ch)])
                      S.add('dve', (lambda sq, ch: lambda e: e.reduce_sum(out=COLQ[:, ch:ch + 1], in_=sq[:, 0:256], axis=AX.X))(sq, ch), reads=[sk], writes=[('COLQ', ch)])
                      cp('pool', gzb(ch), gz[:, 0:256], [gk], gzkeys(ch) + [('GZB', ch)])

                  def v_stats(chs):
                      c0, c1 = chs[0], chs[-1] + 1
                      rs = [('COLS', c) for c in chs]; rq = [('COLQ', c) for c in chs]
                      ts('dve', COLS[:, c0:c1], COLS[:, c0:c1], 1.0 / 256, None, ALU.mult, None, rs, ['MEAN'])
                      tt('dve', COLM[:, c0:c1], COLS[:, c0:c1], COLS[:, c0:c1], ALU.mult, ['MEAN'], ['M2'])
                      stt('dve', COLQ[:, c0:c1], COLQ[:, c0:c1], 1.0 / 256, COLM[:, c0:c1], ALU.mult, ALU.subtract, rq + ['M2'], ['RSTD'])
                      ts('dve', COLQ[:, c0:c1], COLQ[:, c0:c1], 1.0, EPS, ALU.mult, ALU.add, ['RSTD'], ['RSTD'])
                      act(COLQ[:, c0:c1], COLQ[:, c0:c1], AF.Sqrt, ['RSTD'], ['RSTD'])
                      S.add('dve', lambda e: e.reciprocal(out=COLQ[:, c0:c1], in_=COLQ[:, c0:c1]), reads=['RSTD'], writes=['RSTD'])

                  def v_back(ch):
                      t0 = ch * 128
                      i = t0 // 512
                      zb = ch % 3
                      for g in range(4):
                          ts('dve', ZN2[:, zb * 4 + g, (g % 2) * 64:(g % 2) * 64 + 64], gzb(ch)[:, g * 64:(g + 1) * 64], COLS[:, ch:ch + 1], COLQ[:, ch:ch + 1],
                             ALU.subtract, ALU.mult, gzkeys(ch) + [('GZB', ch), 'MEAN', 'RSTD', 'ZN2'], [('ZN2', zb, g)])
                      for j in range(2):
                          pm, pmk = ps_next()
                          mm(pm[:, 0:128], [(ZN2[:, zb * 4 + 2 * j, :], GWT[:, 2 * j, :]), (ZN2[:, zb * 4 + 2 * j + 1, :], GWT[:, 2 * j + 1, :])],
                             [('ZN2', zb, 2 * j), ('ZN2', zb, 2 * j + 1), ('GWT', 2 * j), ('GWT', 2 * j + 1)], pmk)
                          m, mk = tmp_next()
                          stt('dve', m[:, 0:128], pm[:, 0:128], LNC[:, LNG + j:LNG + j + 1], BIAS2[:, j, :], ALU.mult, ALU.add,
                              [pmk, 'LNC', ('BIAS2', j)], [mk])
                          yk = ('Y', j, i)
                          tt('dve', Y[:, j, t0:t0 + 128], Y[:, j, t0:t0 + 128], m[:, 0:128], ALU.mult, [yk, mk], [yk])

                  chs = list(range(st0['v'] // 128, T // 128))
                  for ch in chs:
                      v_front(ch)
                  v_stats(chs)
                  for ch in chs:
                      v_back(ch)

                  chk('A', l, pq)
                  st0['v'] = 96 if l == 0 else 224
                  wB, wBk = load_w(w_in[l].rearrange("(k p) n -> p k n", p=128)[:, :, 512:768], KD, 256)
                  wBg, wBgk = load_w(w_in[l].rearrange("(k p) n -> p k n", p=128)[:, :, 768:1024], KD, 256)
                  for i in range(len(SUBT)):
                      o, n = cur(i)
                      for j in range(2):
                          pg, pgk, _ = proj(wBg, wBgk, i, j)
                          sg, sgk = tmp_next()
                          act(sg[:, 0:n], pg[:, 0:n], AF.Sigmoid, [pgk], [sgk])
                          pa, pak, _ = proj(wB, wBk, i, j)
                          tt('dve', Z[:, j, PAD + o:PAD + o + n], pa[:, 0:n], sg[:, 0:n], ALU.mult, [pak, sgk], [zkey(j, i)])
                  ctiles = [(j, i) for j in range(2) for i in range(len(SUBT))]
                  cst = {}

                  def c_s1(j, i):
                      if i == 0:
                          for q in range(31):
                              ts('dve', DG31[:, q, :], CST[:, IDENT, :], CFW[:, l * 62 + q * 2 + j:l * 62 + q * 2 + j + 1], None, ALU.mult, None,
                                 ['CST', 'CFW'], [('DG31', q)])
                      o, n = cur(i)
                      pc, pck = ps_next()
                      rd = [zkey(j, i), ('Zpad', j)] + ([zkey(j, i - 1)] if i > 0 else []) + [('DG31', q) for q in range(31)]
                      conv_diag(pc, n, pck, [DG31[:, q, :] for q in range(31)],
                                (lambda j, o, n: lambda q: Z[:, j, PAD + o - 30 + q:PAD + o - 30 + q + n])(j, o, n), rd)
                      y, ykk = tmp_next()
                      act(y[:, 0:n], pc[:, 0:n], AF.Identity, [pck, 'LNC'], [ykk], bias=LNC[:, CBD + j:CBD + j + 1], scale=1.0)
                      yb_, ybk = tb_next()
                      cp('dve', yb_[:, 0:n], y[:, 0:n], [ykk], [ybk])
                      cst[(j, i)] = dict(o=o, n=n, y=y, ykk=ykk, yb_=yb_, ybk=ybk)

                  def c_s2(j, i):
                      c = cst[(j, i)]; n = c['n']; y = c['y']; ykk = c['ykk']
                      pmn, pmnk = ps_next()
                      mm(pmn[:, 0:n], [(CSTb[:, GAVG, :], c['yb_'][:, 0:n])], ['CSTb', c['ybk']], pmnk)
                      tt('dve', y[:, 0:n], y[:, 0:n], pmn[:, 0:n], ALU.subtract, [ykk, pmnk], [ykk])
                      sq_, sqk = tb_next()
                      act(sq_[:, 0:n], y[:, 0:n], AF.Square, [ykk], [sqk])
                      c['sq_'] = sq_; c['sqk'] = sqk

                  def c_s3(j, i):
                      c = cst[(j, i)]; o = c['o']; n = c['n']; y = c['y']; ykk = c['ykk']
                      pv, pvk = ps_next()
                      mm(pv[:, 0:n], [(CSTb[:, GAVG, :], c['sq_'][:, 0:n])], ['CSTb', c['sqk']], pvk)
                      r, rk = tmp_next()
                      rsqrt_into(r[:, 0:n], pv[:, 0:n], 1.0, [pvk], rk, n)
                      tt('dve', y[:, 0:n], y[:, 0:n], r[:, 0:n], ALU.mult, [ykk, rk], [ykk])
                      ts('dve', y[:, 0:n], y[:, 0:n], LNC[:, CLG + j:CLG + j + 1], LNC[:, CLB + j:CLB + j + 1], ALU.mult, ALU.add, [ykk, 'LNC'], [ykk])
                      act(Y[:, 2 + j, o:o + n], y[:, 0:n], AF.Silu, [ykk], [('Y', 2 + j, i)])

                  for q in range(len(ctiles) + 2):
                      if q < len(ctiles):
                          c_s1(*ctiles[q])
                      if 1 <= q <= len(ctiles):
                          c_s2(*ctiles[q - 1])
                      if q >= 2:
                          c_s3(*ctiles[q - 2])

                  chk('B', l, pq)
                  for half in range(2):
                      wC1, wC1k = load_w(w_in[l].rearrange("(k p) n -> p k n", p=128)[:, :, 1024 + half * 256:1280 + half * 256], KD, 256)
                      for i in range(len(SUBT)):
                          o, n = cur(i)
                          for c in range(2):
                              pst, pk, _ = proj(wC1, wC1k, i, c)
                              evac_z(pst, pk, n, (2 + c) if half == 0 else c, i)
                  wC2, wC2k = load_w(w_in[l].rearrange("(k p) n -> p k n", p=128)[:, :, 1536:1792], KD, 256)
                  for i in range(len(SUBT)):
                      o, n = cur(i)
                      for j in range(2):
                          pst, pk, _ = proj(wC2, wC2k, i, j)
                          zk_ = zkey(j, i)
                          tt('dve', Z[:, j, PAD + o:PAD + o + n], Z[:, j, PAD + o:PAD + o + n], pst[:, 0:n], ALU.mult, [zk_, pk], [zk_])
                  for i in range(len(SUBT)):
                      o, n = cur(i)
                      for j in range(2):
                          pc, pck = ps_next()
                          rd = [zkey(j, i), ('Zpad', j)] + ([zkey(j, i - 1)] if i > 0 else []) + [('DG3', q, j) for q in range(3)]
                          conv_diag(pc, n, pck, [DG3[:, q * 2 + j, :] for q in range(3)],
                                    (lambda j, o, n: lambda q: Z[:, j, PAD + o - 2 + q:PAD + o - 2 + q + n])(j, o, n), rd)
                          tt('dve', Y[:, 4 + j, o:o + n], Z[:, 2 + j, PAD + o:PAD + o + n], pc[:, 0:n], ALU.mult, [zkey(2 + j, i), pck], [('Y', 4 + j, i)])
                  chk('C', l, pq)
                  wD, wDk = load_w(w_in[l].rearrange("(k p) n -> p k n", p=128)[:, :, 1792:2048], KD, 256)
                  for i in range(len(SUBT)):
                      o, n = cur(i)
                      for j in range(2):
                          pst, pk, _ = proj(wD, wDk, i, j)
                          evac_z(pst, pk, n, j, i)
                  for i in range(len(SUBT)):
                      o, n = cur(i)
                      pls = []
                      for j in range(2):
                          ntap = 4 if j == 0 else 16
                          tb0 = 6 + (0 if j == 0 else 4)
                          pc, pck = ps_next()
                          rd = [zkey(j, i), ('Zpad', j), 'CSTb'] + ([zkey(j, i - 1)] if i > 0 else [])
                          conv_diag(pc, n, pck, [CSTb[:, tb0 + q, :] for q in range(ntap)],
                                    (lambda j, o, n: lambda q: Z[:, j, PAD + o - q:PAD + o - q + n])(j, o, n), rd)
                          plb, plbk = tb_next()
                          if pq == 0 and i == 0:
                              pl, plk = tmp_next()
                              cp('act', pl[:, 0:n], pc[:, 0:n], [pck], [plk])
                              tt('dve', pl[:, HALO - o:HALO - o + 16], pl[:, HALO - o:HALO - o + 16], CNT[:, j, :], ALU.mult, [plk, 'CNT'], [plk])
                              tt('dve', plb[:, 0:n], pl[:, 0:n], Z[:, j, PAD + o:PAD + o + n], ALU.subtract, [plk, zkey(j, i)], [plbk])
                          else:
                              tt('dve', plb[:, 0:n], pc[:, 0:n], Z[:, j, PAD + o:PAD + o + n], ALU.subtract, [pck, zkey(j, i)], [plbk])
                          pw, pwk = ps_next()
                          mm(pw[:, 0:n], [(POOLBD[:, j, :], plb[:, 0:n])], ['POOLBD', plbk], pwk)
                          act(Y[:, 6 + j, o:o + n], pw[:, 0:n], AF.Identity, [pwk, 'LNC'], [('Y', 6 + j, i)], scale=LNC[:, PSC + j:PSC + j + 1], bias=0.0)

                  chk('D', l, pq)
                  st0['v'] = 120 if l == 0 else 248
                  for dc in range(KD):
                      for kb in range(4):
                          c0 = 2048 + kb * 1024 + dc * 128
                          wg, wgk = load_w(w_in[l].rearrange("(k p) n -> p k n", p=128)[:, :, c0:c0 + 128], KD, 128)
                          wb_, wbk = load_w(w_branch[l, kb].rearrange("(j p) d -> p j d", p=128)[:, :, dc * 128:(dc + 1) * 128], 2, 128)
                          for i in range(len(SUBT)):
                              o, n = cur(i)
                              acc, acck = ACC[i], ('acc', i)
                              pg, pgk, _ = proj(wg, wgk, i, 0)
                              sg, sgk = tmp_next()
                              act(sg[:, 0:n], pg[:, 0:n], AF.Sigmoid, [pgk], [sgk])
                              pb, pbk = ps_next()
                              mm(pb[:, 0:n], [(wb_[:, j, :], Y[:, 2 * kb + j, o:o + n]) for j in range(2)],
                                 [wbk, ('Y', 2 * kb, i), ('Y', 2 * kb + 1, i)], pbk)
                              if kb == 0:
                                  tt('dve', acc[:, 0:n], pb[:, 0:n], sg[:, 0:n], ALU.mult, [pbk, sgk], [acck])
                              else:
                                  tt('dve', sg[:, 0:n], pb[:, 0:n], sg[:, 0:n], ALU.mult, [pbk, sgk], [sgk])
                                  if kb < 3:
                                      tt('dve', acc[:, 0:n], acc[:, 0:n], sg[:, 0:n], ALU.add, [acck, sgk], [acck])
                                  else:
                                      tt('dve', MG[:, dc, o:o + n], acc[:, 0:n], sg[:, 0:n], ALU.add, [acck, sgk], [('MG', dc, i)])
                  chk('merge', l, pq)
                  for dc in range(KD):
                      wo, wok = load_w(w_out[l].rearrange("(k p) n -> p k n", p=128)[:, :, dc * 128:(dc + 1) * 128], KD, 128)
                      for i in range(len(SUBT)):
                          o, n = cur(i)
                          pst, pk = ps_next()
                          mm(pst[:, 0:n], [(wo[:, k, :], MG[:, k, o:o + n]) for k in range(KD)], [wok] + [('MG', k, i) for k in range(KD)], pk)
                          tt('dve', X[:, dc, o:o + n], X[:, dc, o:o + n], pst[:, 0:n], ALU.add, [xk(dc, i), pk], [xk(dc, i)])

                  chk('xmid', l, pq)
                  rmsnorm((l * 3 + 1) * KD, 'H', True, pq)
                  for (j0, nj) in JGROUPS:
                      for jj in range(nj):
                          j = j0 + jj
                          wg, wgk = load_w(w_up[l].rearrange("(k p) n -> p k n", p=128)[:, :, j * 128:(j + 1) * 128], KD, 128)
                          wv, wvk = load_w(w_up[l].rearrange("(k p) n -> p k n", p=128)[:, :, DFF + j * 128:DFF + (j + 1) * 128], KD, 128)
                          for i in range(len(SUBT)):
                              o, n = cur(i)
                              res = []
                              for (U, un, cj, ww, wwk) in ((UG, 'UG', j, wg, wgk), (UV, 'UV', NJ + j, wv, wvk)):
                                  pst, pk, _ = proj(ww, wwk, i, 0)
                                  cp('act', U[:, PAD + o:PAD + o + n], pst[:, 0:n], [pk], [(un, i)])
                                  a, ak = tmp_next()
                                  wbase = l * 132 + cj
                                  rd = [(un, i), un + 'pad', 'FWC', 'FBC'] + ([(un, i - 1)] if i > 0 else [])
                                  if un == 'UG':
                                      act(a[:, 0:n], pst[:, 0:n], AF.Identity, [pk, 'FWC', 'FBC'], [ak],
                                          scale=FWC[:, wbase + 88:wbase + 89], bias=FBC[:, l * 44 + cj:l * 44 + cj + 1])
                                  else:
                                      ts('pool', a[:, 0:n], U[:, PAD + o:PAD + o + n], FWC[:, wbase + 88:wbase + 89], FBC[:, l * 44 + cj:l * 44 + cj + 1],
                                         ALU.mult, ALU.add, rd, [ak])
                                  stt('dve', a[:, 0:n], U[:, PAD + o - 1:PAD + o - 1 + n], FWC[:, wbase + 44:wbase + 45], a[:, 0:n], ALU.mult, ALU.add, rd + [ak], [ak])
                                  stt('dve', a[:, 0:n], U[:, PAD + o - 2:PAD + o - 2 + n], FWC[:, wbase:wbase + 1], a[:, 0:n], ALU.mult, ALU.add, rd + [ak], [ak])
                                  res.append((a, ak))
                              (ga, gak), (va, vak) = res
                              sg, sgk = tmp_next()
                              act(sg[:, 0:n], ga[:, 0:n], AF.Silu, [gak], [sgk])
                              tt('dve', Y[:, jj, o:o + n], sg[:, 0:n], va[:, 0:n], ALU.mult, [sgk, vak], [('Y', jj, i)])
                      for dc in range(KD):
                          wd, wdk = load_w(w_down[l].rearrange("(k p) n -> p k n", p=128)[:, j0:j0 + nj, dc * 128:(dc + 1) * 128], nj, 128)
                          for i in range(len(SUBT)):
                              o, n = cur(i)
                              pst, pk = ps_next()
                              mm(pst[:, 0:n], [(wd[:, jj, :], Y[:, jj, o:o + n]) for jj in range(nj)], [wdk] + [('Y', jj, i) for jj in range(nj)], pk)
                              tt('dve', X[:, dc, o:o + n], X[:, dc, o:o + n], pst[:, 0:n], ALU.add, [xk(dc, i), pk], [xk(dc, i)])

                  chk('ffn', l, pq)
                  rmsnorm((l * 3 + 2) * KD, 'H', False)
                  ptkeys = [zkey(jz, iz) for jz in range(2) for iz in range(len(SUBT))]
                  cdma(Z[:, 0:2, PAD:PAD + T], pT[pq, l].rearrange("(j p) t -> p j t", p=128), ptkeys, 'PT')
                  for dc in range(KD):
                      wg, wgk = load_w(w_pg[l].rearrange("(k p) n -> p k n", p=128)[:, :, dc * 128:(dc + 1) * 128], KD, 128)
                      wp, wpk = load_w(w_pp[l].rearrange("(j p) n -> p j n", p=128)[:, :, dc * 128:(dc + 1) * 128], 2, 128)
                      for i in range(len(SUBT)):
                          o, n = cur(i)
                          pg, pgk, _ = proj(wg, wgk, i, 0)
                          sg, sgk = tmp_next()
                          act(sg[:, 0:n], pg[:, 0:n], AF.Sigmoid, [pgk], [sgk])
                          pp, ppk = ps_next()
                          mm(pp[:, 0:n], [(wp[:, j, :], Z[:, j, PAD + o:PAD + o + n]) for j in range(2)], [wpk, zkey(0, i), zkey(1, i)], ppk)
                          tt('dve', sg[:, 0:n], pp[:, 0:n], sg[:, 0:n], ALU.mult, [ppk, sgk], [sgk])
                          tt('dve', X[:, dc, o:o + n], X[:, dc, o:o + n], sg[:, 0:n], ALU.add, [xk(dc, i), sgk], [xk(dc, i)])

              chk('end', L - 1, pq)
              st0['v'] = 0
              fin_sub = [(256, 256, 0), (512, 512, 1), (1024, 256, 2)]
              for (o, n, i) in fin_sub:
                  pst, pk = ps_next()
                  for k in range(KD):
                      tb, tk = tb_next()
                      act(tb[:, 0:n], X[:, k, o:o + n], AF.Square, [xk(k, i)], [tk])
                      S.add('pe', (lambda tb, k, pst, o, n: lambda e: e.matmul(pst[:, 0:n], lhsT=CSTb[:, ONES, :], rhs=tb[:, 0:n], start=(k == 0), stop=(k == KD - 1)))(tb, k, pst, o, n),
                            reads=[tk, 'CSTb'], writes=[pk])
                  r, rk = tmp_next()
                  rsqrt_into(r[:, 0:n], pst[:, 0:n], 1.0 / D, [pk], rk, n)
                  for k in range(KD):
                      ot, otk = tmp_next()
                      stt('dve', ot[:, 0:n], X[:, k, o:o + n], GC[:, L * 3 * KD + k:L * 3 * KD + k + 1], r[:, 0:n], ALU.mult, ALU.mult, [xk(k, i), rk, 'GC'], [otk])
                      okey = ('out', pq, k, o)
                      S.add('sp', (lambda ot, k, o, n, pq: lambda e: e.dma_start(out=outT[pq, k * 128:(k + 1) * 128, o - 256:o - 256 + n], in_=ot[:, 0:n]))(ot, k, o, n, pq),
                            reads=[otk], writes=[okey], dma_key=('o', k))
                      outs.append(S.ops[-1])

        except _Stop:
            pass
        if debug is not None:
            allkeys = list(S.last_writer.keys())
            for nm, (dap, tns) in dbg.items():
                S.add('sp', (lambda dap, tns: lambda e: e.dma_start(out=dap[:], in_=tns[:]))(dap, tns), reads=allkeys, writes=[('dbg', nm)], dma_key=('dbg', nm))
                outs.append(S.ops[-1])
        S.finalize(nc, st)
        with nc.Block() as block0:
            @block0.gpsimd
            def _(e):
                for sem in S.sems.values():
                    e.sem_clear(sem)
                for sem in S.sems.values():
                    e.wait_op(sem, 0, "sem-eq")
        with nc.Block() as block:
            @block.sync
            def _(e):
                fw = {}
                for op in outs:
                    fw[op.token[0]] = max(fw.get(op.token[0], 0), op.token[1])
                S.emit_engine('sp', e, list(fw.items()))

            @block.scalar
            def _(e): S.emit_engine('act', e)

            @block.vector
            def _(e): S.emit_engine('dve', e)

            @block.gpsimd
            def _(e): S.emit_engine('pool', e)

            @block.tensor
            def _(e): S.emit_engine('pe', e)
    return nc


def _cols(v):
    return np.ascontiguousarray(np.asarray(v, np.float32).reshape(-1, 128).T)


def prepare(x, p, g_mix, w_in, gmlp_ln_g, gmlp_ln_b, gmlp_w_s, gmlp_b_s,
           conf_w_dw, conf_b_dw, conf_ln_g, conf_ln_b, short_w, pool_w, pool_scale,
           w_branch, w_out, g_ffn, ffn_w_up, ffn_w_conv, ffn_b_conv, ffn_w_down,
           g_ple, ple_w_gate, ple_w_proj, g_final):
    f = lambda a: np.ascontiguousarray(np.asarray(a, np.float32))
    x = f(x); p = f(p)
    B, SEQ, _ = x.shape
    gcols = np.concatenate([_cols(v[l]) for l in range(L) for v in (g_mix, g_ffn, g_ple)] + [_cols(g_final)], axis=1)
    lncols = np.concatenate([_cols(np.asarray(v)[l].reshape(-1)) for l in range(L)
                             for v in (gmlp_ln_g, gmlp_ln_b, conf_b_dw, conf_ln_g, conf_ln_b, pool_scale)], axis=1)
    shortw = np.concatenate([_cols(np.asarray(short_w)[l, q]) for l in range(L) for q in range(3)], axis=1)
    confw = np.concatenate([_cols(np.asarray(conf_w_dw)[l, q]) for l in range(L) for q in range(31)], axis=1)
    ffnwc = np.concatenate([_cols(np.asarray(ffn_w_conv)[l, q]) for l in range(L) for q in range(3)], axis=1)
    ffnbc = np.concatenate([_cols(np.asarray(ffn_b_conv)[l]) for l in range(L)], axis=1)
    gwT = f(np.transpose(np.asarray(gmlp_w_s), (0, 1, 3, 2)))
    gbs = f(np.asarray(gmlp_b_s).reshape(1, -1))
    pw = np.asarray(pool_w, np.float32)
    poolbd = np.zeros((L, 2, 128, 128), np.float32)
    for l in range(L):
        for j in range(2):
            poolbd[l, j, :64, :64] = pw[l, 2 * j]
            poolbd[l, j, 64:, 64:] = pw[l, 2 * j + 1]
    consts = np.zeros((26, 128, 128), np.float32)
    consts[0] = np.eye(128)
    consts[1] = np.triu(np.ones((128, 128)))
    consts[2][:, :64] = 1.0
    consts[3][:, 64:] = 1.0
    consts[4][:64, :64] = 1.0 / 64; consts[4][64:, 64:] = 1.0 / 64
    consts[5] = 1.0
    pidx = np.arange(128)
    for j in range(2):
        ntap = 4 if j == 0 else 16
        tb0 = 6 + (0 if j == 0 else 4)
        win = np.where(pidx < 64, WINS[2 * j], WINS[2 * j + 1]).astype(np.float32)
        for q in range(ntap):
            consts[tb0 + q][pidx, pidx] = np.where(q < win, 1.0 / win, 0.0)
    in_maps = []
    for c in range(8):
        b, seg = c // 4, c % 4
        xT = np.zeros((NPASS, D, T), np.float32)
        pT = np.zeros((NPASS, L, 256, T), np.float32)
        mask = np.full((128, HALO), 0.0 if seg == 0 else 1.0, np.float32)
        cnt = np.ones((128, 2, 16), np.float32)
        tpos16 = seg * 2048 + np.arange(16)
        for j in range(2):
            win = np.where(pidx < 64, WINS[2 * j], WINS[2 * j + 1]).astype(np.float32)[:, None]
            cnt[:, j, :] = win / np.minimum(tpos16[None, :] + 1, win)
        for q in range(NPASS):
            s = seg * 2048 + q * TOUT - HALO
            lo = max(s, 0)
            xT[q, :, lo - s:] = x[b, lo:s + T].T
            for l in range(L):
                pT[q, l, :, lo - s:] = p[l, b, lo:s + T].T
        in_maps.append({
            "xT": xT, "pT": pT, "mask": mask, "cnt": cnt,
            "w_in": f(w_in), "w_branch": f(w_branch), "w_out": f(w_out), "w_up": f(ffn_w_up), "w_down": f(ffn_w_down),
            "w_pg": f(ple_w_gate), "w_pp": f(ple_w_proj), "gcols": f(gcols), "lncols": f(lncols), "shortw": f(shortw),
            "confw": f(confw), "ffnwc": f(ffnwc), "ffnbc": f(ffnbc), "gwT": gwT, "gbs": gbs, "poolbd": poolbd, "consts": consts,
        })
    return in_maps


def kernel(**inputs):
    in_maps = prepare(**inputs)
    B, SEQ = 2, 8192
    nc = build_nc()
    res = run_bass_kernel_spmd(nc, in_maps, core_ids=list(range(8)))
    out = np.zeros((B, SEQ, D), np.float32)
    for c in range(8):
        b, seg = c // 4, c % 4
        o = res.results[c]["outT"]
        for q in range(NPASS):
            s = seg * 2048 + q * TOUT
            out[b, s:s + TOUT, :] = o[q].T
    return out
```

```python
from contextlib import ExitStack
import numpy as np
import concourse.bass as bass
import concourse.mybir as mybir
from concourse.bass_utils import run_bass_kernel_spmd

F32 = mybir.dt.float32
BF16 = mybir.dt.bfloat16
AF = mybir.ActivationFunctionType
ALU = mybir.AluOpType
AX = mybir.AxisListType

L = 2
D = 1024
KD = 8
HALO = 256
TOUT = 1024
T = HALO + TOUT
PAD = 32
NPASS = 2
SUBT = [(0, 512), (512, 512), (1024, 256)]
DFF = 2816
NJ = 22
JGROUPS = [(0, 8), (8, 8), (16, 6)]
EPS = 1e-6
WINS = (2, 4, 8, 16)


class _Op:
    __slots__ = ('eng', 'emit', 'is_dma', 'dma_key', 'deps', 'signal', 'token', 'waits', 'idx', 'snapshot', 'stage', 'hoist')


class Sched:
    def __init__(self):
        self.ops = []
        self.last_writer = {}
        self.readers = {}
        self.stage = ''

    def add(self, eng, emit, reads=(), writes=(), dma_key=None, hoist=False):
        op = _Op()
        op.hoist = hoist
        op.eng = eng
        op.emit = emit
        op.is_dma = dma_key is not None
        op.dma_key = dma_key
        op.signal = op.is_dma
        op.token = None
        op.idx = len(self.ops)
        op.stage = self.stage
        deps = {}
        for k in reads:
            w = self.last_writer.get(k)
            if w is not None:
                deps[w.idx] = (w, True)
        for k in writes:
            w = self.last_writer.get(k)
            if w is not None:
                deps[w.idx] = (w, True)
            for r in self.readers.get(k, ()):
                if r.idx not in deps:
                    deps[r.idx] = (r, False)
        for k in reads:
            self.readers.setdefault(k, []).append(op)
        for k in writes:
            self.last_writer[k] = op
            self.readers[k] = []
        op.deps = []
        for i in sorted(deps):
            d, hard = deps[i]
            if not d.is_dma and not op.is_dma and d.eng == op.eng:
                if op.eng == 'pe' or not hard:
                    continue
            d.signal = True
            op.deps.append(d)
        self.ops.append(op)
        return op

    def finalize(self, nc, stack):
        keyed = []
        for op in self.ops:
            if op.hoist:
                m = max([d.idx for d in op.deps], default=-1)
                keyed.append(((m, 1, op.idx), op))
            else:
                keyed.append(((op.idx, 0, 0), op))
        keyed.sort(key=lambda kv: kv[0])
        self.ops = [op for _, op in keyed]
        self.sems = {}
        cnt = {}
        for op in self.ops:
            if not op.signal:
                continue
            key = ('dma', op.dma_key) if op.is_dma else ('eng', op.eng)
            if key not in self.sems:
                self.sems[key] = stack.enter_context(nc.semaphore("s%d" % len(self.sems)))
                cnt[key] = 0
            cnt[key] += 16 if op.is_dma else 1
            op.token = (key, cnt[key])
        known = {}
        for op in self.ops:
            K = known.setdefault(op.eng, {})
            waits = {}
            for d in op.deps:
                key, val = d.token
                if K.get(key, 0) >= val:
                    continue
                waits[key] = max(waits.get(key, 0), val)
                for s, v in d.snapshot.items():
                    if K.get(s, 0) < v:
                        K[s] = v
                K[key] = val
            op.waits = list(waits.items())
            op.snapshot = dict(K)

    def emit_engine(self, name, eng, final_waits=()):
        for op in self.ops:
            if op.eng != name:
                continue
            for key, val in op.waits:
                eng.wait_ge(self.sems[key], val)
            ins = op.emit(eng)
            if op.signal:
                ins.then_inc(self.sems[op.token[0]], 16 if op.is_dma else 1)
        for key, val in final_waits:
            eng.wait_ge(self.sems[key], val)


class _Stop(Exception):
    pass


def build_nc(debug=None):
    nc = bass.Bass("TRN2", target_bir_lowering=False)

    def din(name, shape):
        return nc.dram_tensor(name, list(shape), F32, kind="ExternalInput").ap()

    xT = din("xT", [NPASS, D, T])
    pT = din("pT", [NPASS, L, 256, T])
    maskd = din("mask", [128, HALO])
    cntd = din("cnt", [128, 2, 16])
    w_in = din("w_in", [L, D, 6144])
    w_branch = din("w_branch", [L, 4, 256, D])
    w_out = din("w_out", [L, D, D])
    w_up = din("w_up", [L, D, 2 * DFF])
    w_down = din("w_down", [L, DFF, D])
    w_pg = din("w_pg", [L, D, D])
    w_pp = din("w_pp", [L, 256, D])
    gcols_d = din("gcols", [128, L * 3 * KD + KD])
    lncols_d = din("lncols", [128, L * 6 * 2])
    shortw_d = din("shortw", [128, L * 3 * 2])
    confw_d = din("confw", [128, L * 31 * 2])
    ffnwc_d = din("ffnwc", [128, L * 3 * 44])
    ffnbc_d = din("ffnbc", [128, L * 44])
    gwT_d = din("gwT", [L, 4, 128, 128])
    gbs_d = din("gbs", [1, L * 4 * 128])
    poolbd_d = din("poolbd", [L, 2, 128, 128])
    consts_d = din("consts", [6 + 20, 128, 128])
    outT = nc.dram_tensor("outT", [NPASS, D, TOUT], F32, kind="ExternalOutput").ap()

    S = Sched()
    st = ExitStack()
    with st:
        def sb(name, shape, dt=F32):
            return st.enter_context(nc.sbuf_tensor(name, list(shape), dt))

        X = sb("X", [128, KD, T])
        H = sb("H", [128, KD, T], BF16)
        Z = sb("Z", [128, 4, PAD + T], BF16)
        Y = sb("Y", [128, 8, T], BF16)
        MG = sb("MG", [128, KD, T], BF16)
        UG = sb("UG", [128, PAD + T])
        UV = sb("UV", [128, PAD + T])
        MASK = sb("MASK", [128, HALO])
        CNT = sb("CNT", [128, 2, 16])
        ZN2 = sb("ZN2", [128, 12, 128], BF16)
        BIAS2 = sb("BIAS2", [128, 2, 128])
        GWT = sb("GWT", [128, 4, 128], BF16)
        GWTs = sb("GWTs", [128, 4, 128], BF16)
        GBSb = sb("GBSb", [1, L * 4 * 128], BF16)
        POOLBD = sb("POOLBD", [128, 2, 128], BF16)
        CST = sb("CST", [128, 2, 128])
        CSTb = sb("CSTb", [128, 26, 128], BF16)
        DG31 = sb("DG31", [128, 31, 128], BF16)
        DG3 = sb("DG3", [128, 6, 128], BF16)
        GC = sb("GC", [128, L * 3 * KD + KD])
        LNC = sb("LNC", [128, L * 12])
        SHW = sb("SHW", [128, L * 6])
        CFW = sb("CFW", [128, L * 62])
        FWC = sb("FWC", [128, L * 132])
        FBC = sb("FBC", [128, L * 44])
        EPSC = sb("EPSC", [128, 1])
        NT = 8
        TMP = [sb("TMP%d" % i, [128, 512]) for i in range(NT)]
        TB = [sb("TB%d" % i, [128, 512], BF16) for i in range(4)]
        COL = sb("COL", [128, 8])
        COLS = sb("COLS", [128, 16])
        COLQ = sb("COLQ", [128, 16])
        COLM = sb("COLM", [128, 16])
        NWB = 5
        WB = [sb("WB%d" % i, [128, 2048], BF16) for i in range(NWB)]
        ACC = [sb("ACC%d" % i, [128, 512]) for i in range(len(SUBT))]
        PS = [st.enter_context(nc.psum_tensor("PS%d" % i, [128, 512], F32)) for i in range(8)]

        IDENT, TRI, SELA, SELB, GAVG, ONES = 0, 1, 2, 3, 4, 5
        state = {'ps': 0, 'ws': 0, 'wb': 0, 'tmp': 0, 'tb': 0, 'dmaq': 0}

        def ps_next():
            i = state['ps']; state['ps'] = (i + 1) % 8
            return PS[i], ('ps', i)

        def tmp_next():
            i = state['tmp']; state['tmp'] = (i + 1) % NT
            return TMP[i], ('tmp', i)

        def tb_next():
            i = state['tb']; state['tb'] = (i + 1) % 4
            return TB[i], ('tb', i)

        def dma(out_ap, in_ap, wkey, reads=()):
            S.add('sp', lambda e: e.dma_start(out=out_ap, in_=in_ap), reads=list(reads), writes=[wkey], dma_key=wkey)

        def load_w(src3, kc, ncol):
            j = state['wb']; state['wb'] = (j + 1) % NWB
            n = kc * ncol
            bview = WB[j][:, 0:n].rearrange("p (k n) -> p k n", k=kc)
            cdma(bview, src3, [('wb', j)], ('wb', j), hoist=True)
            return bview, ('wb', j)

        def cdma(out_ap, in_ap, wkeys, dkey, hoist=False):
            S.add('pool', lambda e: e.dma_start(out=out_ap, in_=in_ap), writes=list(wkeys), dma_key=dkey, hoist=hoist)

        def mm(out_ap, pairs, reads, wkey):
            def f(e):
                n = len(pairs)
                for q, (l, r) in enumerate(pairs):
                    ins = e.matmul(out_ap, lhsT=l, rhs=r, start=(q == 0), stop=(q == n - 1))
                return ins
            S.add('pe', f, reads=list(reads), writes=[wkey])

        def act(out, in_, func, reads, writes, **kw):
            S.add('act', lambda e: e.activation(out=out, in_=in_, func=func, **kw), reads=list(reads), writes=list(writes))

        def tt(eng, out, a, b, op, reads, writes):
            S.add(eng, lambda e: e.tensor_tensor(out=out, in0=a, in1=b, op=op), reads=list(reads), writes=list(writes))

        def ts(eng, out, a, s1, s2, op0, op1, reads, writes):
            if s2 is None:
                S.add(eng, lambda e: e.tensor_scalar(out=out, in0=a, scalar1=s1, scalar2=None, op0=op0), reads=list(reads), writes=list(writes))
            else:
                S.add(eng, lambda e: e.tensor_scalar(out=out, in0=a, scalar1=s1, scalar2=s2, op0=op0, op1=op1), reads=list(reads), writes=list(writes))

        def stt(eng, out, a, sc, b, op0, op1, reads, writes):
            S.add(eng, lambda e: e.scalar_tensor_tensor(out=out, in0=a, scalar=sc, in1=b, op0=op0, op1=op1), reads=list(reads), writes=list(writes))

        def cp(eng, out, in_, reads, writes):
            if eng == 'act':
                S.add(eng, lambda e: e.activation(out=out, in_=in_, func=AF.Copy), reads=list(reads), writes=list(writes))
            else:
                S.add(eng, lambda e: e.tensor_copy(out=out, in_=in_), reads=list(reads), writes=list(writes))

        def rsqrt_into(out, in_ap, scale, reads, wkey, n):
            ts('dve', out, in_ap, scale, EPS, ALU.mult, ALU.add, reads, [wkey])
            act(out, out, AF.Sqrt, [wkey], [wkey])
            S.add('dve', lambda e: e.reciprocal(out=out, in_=out), reads=[wkey], writes=[wkey])

        dma(CST[:], consts_d[0:2].rearrange("c p n -> p c n"), 'CST')
        cdma(CSTb[:], consts_d.rearrange("c p n -> p c n"), ['CSTb'], 'CSTb')
        dma(GC[:], gcols_d, 'GC'); dma(LNC[:], lncols_d, 'LNC'); dma(SHW[:], shortw_d, 'SHW')
        dma(CFW[:], confw_d, 'CFW'); dma(FWC[:], ffnwc_d, 'FWC'); dma(FBC[:], ffnbc_d, 'FBC')
        cdma(GBSb[:], gbs_d, ['GBSb'], 'GBSb')
        dma(MASK[:], maskd, 'MASK')
        dma(CNT[:], cntd, 'CNT')
        S.add('pool', lambda e: e.memset(EPSC[:], EPS), writes=['EPSC'])
        S.add('pool', lambda e: e.memset(ZN2[:], 0.0), writes=['ZN2'])
        for zi in range(4):
            S.add('pool', (lambda zi: lambda e: e.memset(Z[:, zi, 0:PAD], 0.0))(zi), writes=[('Zpad', zi)])
        S.add('pool', lambda e: e.memset(UG[:, 0:PAD], 0.0), writes=['UGpad'])
        S.add('pool', lambda e: e.memset(UV[:, 0:PAD], 0.0), writes=['UVpad'])

        def xk(k, i): return ('X', k, i)

        st0 = {'v': 0}

        def cur(i):
            o, n = SUBT[i]
            if i == 0:
                return st0['v'], n - st0['v']
            return o, n

        def rmsnorm(gbase, hk, masked, pq=0):
            for i in range(len(SUBT)):
                o, n = cur(i)
                pst, pk = ps_next()
                tbs = []
                for k in range(KD):
                    tb, tk = tb_next()
                    act(tb[:, 0:n], X[:, k, o:o + n], AF.Square, [xk(k, i)], [tk])
                    tbs.append((tb, tk))
                    if len(tbs) == 4 or k == KD - 1:
                        k0 = k - len(tbs) + 1
                        for q, (tb2, tk2) in enumerate(tbs):
                            kk = k0 + q
                            S.add('pe', (lambda tb2, kk, pst, n: lambda e: e.matmul(pst[:, 0:n], lhsT=CSTb[:, ONES, :], rhs=tb2[:, 0:n], start=(kk == 0), stop=(kk == KD - 1)))(tb2, kk, pst, n),
                                  reads=[tk2, 'CSTb'], writes=[pk])
                        tbs = []
                r, rk = tmp_next()
                rsqrt_into(r[:, 0:n], pst[:, 0:n], 1.0 / D, [pk], rk, n)
                if masked and pq == 0 and i == 0:
                    tt('dve', r[:, 0:HALO - o], r[:, 0:HALO - o], MASK[:, o:HALO], ALU.mult, [rk, 'MASK'], [rk])
                for k in range(KD):
                    stt('dve', H[:, k, o:o + n], X[:, k, o:o + n], GC[:, gbase + k:gbase + k + 1], r[:, 0:n], ALU.mult, ALU.mult,
                        [xk(k, i), rk, 'GC'], [(hk, k, i)])

        def hreads(i): return [('H', k, i) for k in range(KD)]

        def proj(wcols, wk, i, c):
            o, n = cur(i)
            pst, pk = ps_next()
            mm(pst[:, 0:n], [(wcols[:, k, c * 128:(c + 1) * 128], H[:, k, o:o + n]) for k in range(KD)], hreads(i) + [wk], pk)
            return pst, pk, n

        def conv_diag(pst, n, pk, mats, src_fn, reads):
            mm(pst[:, 0:n], [(mats[q], src_fn(q)) for q in range(len(mats))], reads, pk)

        outs = []

        def chk(tag, l, pq):
            S.stage = 'after_%s_L%d_P%d' % (tag, l, pq)
            if debug is not None and debug == (tag, l) and pq == 0:
                raise _Stop()

        dbg = {}
        if debug is not None:
            for nm, tns, dt in (("dX", X, F32), ("dH", H, BF16), ("dY", Y, BF16), ("dMG", MG, BF16), ("dZ", Z, BF16), ("dUG", UG, F32)):
                dbg[nm] = (nc.dram_tensor(nm, list(tns.shape), dt, kind="ExternalOutput").ap(), tns)
        try:
          for pq in range(NPASS):
              for k in range(KD):
                  for i, (o, n) in enumerate(SUBT):
                      dma(X[:, k, o:o + n], xT[pq, k * 128:(k + 1) * 128, o:o + n], xk(k, i))
              for l in range(L):
                  lc = l * 12
                  LNG, LNB, CBD, CLG, CLB, PSC = [lc + 2 * q for q in range(6)]
                  cdma(GWTs[:], gwT_d[l].rearrange("g s t -> s g t"), ['GWTs'], 'GWTs')
                  for g in range(4):
                      tt('dve', GWT[:, g, :], GWTs[:, g, :], CSTb[:, TRI, :], ALU.mult, ['GWTs', 'CSTb'], [('GWT', g)])
                  cdma(POOLBD[:], poolbd_d[l].rearrange("j c d -> c j d"), ['POOLBD'], 'POOLBD')
                  for q in range(3):
                      for j in range(2):
                          ts('dve', DG3[:, q * 2 + j, :], CST[:, IDENT, :], SHW[:, l * 6 + q * 2 + j:l * 6 + q * 2 + j + 1], None, ALU.mult, None,
                             ['CST', 'SHW'], [('DG3', q, j)])
                  for j in range(2):
                      p1, k1 = ps_next()
                      mm(p1[:, 0:128], [(CSTb[:, SELA, :], GWT[:, 2 * j, :]), (CSTb[:, SELB, :], GWT[:, 2 * j + 1, :])],
                         ['CSTb', ('GWT', 2 * j), ('GWT', 2 * j + 1)], k1)
                      p2, k2 = ps_next()
                      b0 = (l * 4 + 2 * j) * 128
                      mm(p2[:, 0:128], [(CSTb[0:1, SELA, :], GBSb[0:1, b0:b0 + 128]), (CSTb[0:1, SELB, :], GBSb[0:1, b0 + 128:b0 + 256])],
                         ['CSTb', 'GBSb'], k2)
                      t2, tk2 = tmp_next()
                      cp('act', t2[:, 0:128], p2[:, 0:128], [k2], [tk2])
                      stt('dve', BIAS2[:, j, :], p1[:, 0:128], LNC[:, LNB + j:LNB + j + 1], t2[:, 0:128], ALU.mult, ALU.add,
                          [k1, tk2, 'LNC'], [('BIAS2', j)])

                  st0['v'] = 0 if l == 0 else 128
                  rmsnorm((l * 3 + 0) * KD, 'H', True, pq)

                  def zkey(zi, i): return ('Z', zi, i)

                  def evac_z(pst, pk, n, zi, i):
                      o = cur(i)[0]
                      cp('act', Z[:, zi, PAD + o:PAD + o + n], pst[:, 0:n], [pk], [zkey(zi, i)])

                  def gelu_from_psum(pst, pk, n, out_ap, wkeys):
                      act(out_ap, pst[:, 0:n], AF.Gelu_apprx_tanh, [pk], wkeys)

                  chk('h', l, pq)
                  wA, wAk = load_w(w_in[l].rearrange("(k p) n -> p k n", p=128)[:, :, 0:256], KD, 256)
                  wV, wVk = load_w(w_in[l].rearrange("(k p) n -> p k n", p=128)[:, :, 256:512], KD, 256)
                  st0['v'] = 126 if l == 0 else 254
                  for i in range(len(SUBT)):
                      o, n = cur(i)
                      for j in range(2):
                          pst, pk, _ = proj(wA, wAk, i, j)
                          gelu_from_psum(pst, pk, n, Y[:, j, o:o + n], [('Y', j, i)])
                  st0['v'] = 0 if l == 0 else 128
                  def gzb(ch):
                      return Z[:, ch // 5, PAD + (ch % 5) * 256:PAD + (ch % 5) * 256 + 256]

                  def gzkeys(ch):
                      return [zkey(ch // 5, iz) for iz in range(len(SUBT))]

                  def v_front(ch):
                      t0 = ch * 128
                      i = t0 // 512
                      pst, pk = ps_next()
                      mm(pst[:, 0:256], [(H[:, k, t0:t0 + 128], wV[:, k, 0:256]) for k in range(KD)], hreads(i) + [wVk], pk)
                      gz, gk = tmp_next()
                      gelu_from_psum(pst, pk, 256, gz[:, 0:256], [gk])
                      sq, sk = tmp_next()
                      act(sq[:, 0:256], gz[:, 0:256], AF.Square, [gk], [sk])
                      S.add('dve', (lambda gz, ch: lambda e: e.reduce_sum(out=COLS[:, ch:ch + 1], in_=gz[:, 0:256], axis=AX.X))(gz, ch), reads=[gk], writes=[('COLS', ch)])
                      S.add('dve', (lambda sq, ch: lambda e: e.reduce_sum(out=COLQ[:, ch:ch + 1], in_=sq[:, 0:256], axis=AX.X))(sq, ch), reads=[sk], writes=[('COLQ', ch)])
                      cp('pool', gzb(ch), gz[:, 0:256], [gk], gzkeys(ch) + [('GZB', ch)])

                  def v_stats(chs):
                      c0, c1 = chs[0], chs[-1] + 1
                      rs = [('COLS', c) for c in chs]; rq = [('COLQ', c) for c in chs]
                      ts('dve', COLS[:, c0:c1], COLS[:, c0:c1], 1.0 / 256, None, ALU.mult, None, rs, ['MEAN'])
                      tt('dve', COLM[:, c0:c1], COLS[:, c0:c1], COLS[:, c0:c1], ALU.mult, ['MEAN'], ['M2'])
                      stt('dve', COLQ[:, c0:c1], COLQ[:, c0:c1], 1.0 / 256, COLM[:, c0:c1], ALU.mult, ALU.subtract, rq + ['M2'], ['RSTD'])
                      ts('dve', COLQ[:, c0:c1], COLQ[:, c0:c1], 1.0, EPS, ALU.mult, ALU.add, ['RSTD'], ['RSTD'])
                      act(COLQ[:, c0:c1], COLQ[:, c0:c1], AF.Sqrt, ['RSTD'], ['RSTD'])
                      S.add('dve', lambda e: e.reciprocal(out=COLQ[:, c0:c1], in_=COLQ[:, c0:c1]), reads=['RSTD'], writes=['RSTD'])

                  def v_back(ch):
                      t0 = ch * 128
                      i = t0 // 512
                      zb = ch % 3
                      for g in range(4):
                          ts('dve', ZN2[:, zb * 4 + g, (g % 2) * 64:(g % 2) * 64 + 64], gzb(ch)[:, g * 64:(g + 1) * 64], COLS[:, ch:ch + 1], COLQ[:, ch:ch + 1],
                             ALU.subtract, ALU.mult, gzkeys(ch) + [('GZB', ch), 'MEAN', 'RSTD', 'ZN2'], [('ZN2', zb, g)])
                      for j in range(2):
                          pm, pmk = ps_next()
                          mm(pm[:, 0:128], [(ZN2[:, zb * 4 + 2 * j, :], GWT[:, 2 * j, :]), (ZN2[:, zb * 4 + 2 * j + 1, :], GWT[:, 2 * j + 1, :])],
                             [('ZN2', zb, 2 * j), ('ZN2', zb, 2 * j + 1), ('GWT', 2 * j), ('GWT', 2 * j + 1)], pmk)
                          m, mk = tmp_next()
                          stt('dve', m[:, 0:128], pm[:, 0:128], LNC[:, LNG + j:LNG + j + 1], BIAS2[:, j, :], ALU.mult, ALU.add,
                              [pmk, 'LNC', ('BIAS2', j)], [mk])
                          yk = ('Y', j, i)
                          tt('dve', Y[:, j, t0:t0 + 128], Y[:, j, t0:t0 + 128], m[:, 0:128], ALU.mult, [yk, mk], [yk])

                  chs = list(range(st0['v'] // 128, T // 128))
                  for ch in chs:
                      v_front(ch)
                  v_stats(chs)
                  for ch in chs:
                      v_back(ch)

                  chk('A', l, pq)
                  st0['v'] = 96 if l == 0 else 224
                  wB, wBk = load_w(w_in[l].rearrange("(k p) n -> p k n", p=128)[:, :, 512:768], KD, 256)
                  wBg, wBgk = load_w(w_in[l].rearrange("(k p) n -> p k n", p=128)[:, :, 768:1024], KD, 256)
                  for i in range(len(SUBT)):
                      o, n = cur(i)
                      for j in range(2):
                          pg, pgk, _ = proj(wBg, wBgk, i, j)
                          sg, sgk = tmp_next()
                          act(sg[:, 0:n], pg[:, 0:n], AF.Sigmoid, [pgk], [sgk])
                          pa, pak, _ = proj(wB, wBk, i, j)
                          tt('dve', Z[:, j, PAD + o:PAD + o + n], pa[:, 0:n], sg[:, 0:n], ALU.mult, [pak, sgk], [zkey(j, i)])
                  ctiles = [(j, i) for j in range(2) for i in range(len(SUBT))]
                  cst = {}

                  def c_s1(j, i):
                      if i == 0:
                          for q in range(31):
                              ts('dve', DG31[:, q, :], CST[:, IDENT, :], CFW[:, l * 62 + q * 2 + j:l * 62 + q * 2 + j + 1], None, ALU.mult, None,
                                 ['CST', 'CFW'], [('DG31', q)])
                      o, n = cur(i)
                      pc, pck = ps_next()
                      rd = [zkey(j, i), ('Zpad', j)] + ([zkey(j, i - 1)] if i > 0 else []) + [('DG31', q) for q in range(31)]
                      conv_diag(pc, n, pck, [DG31[:, q, :] for q in range(31)],
                                (lambda j, o, n: lambda q: Z[:, j, PAD + o - 30 + q:PAD + o - 30 + q + n])(j, o, n), rd)
                      y, ykk = tmp_next()
                      act(y[:, 0:n], pc[:, 0:n], AF.Identity, [pck, 'LNC'], [ykk], bias=LNC[:, CBD + j:CBD + j + 1], scale=1.0)
                      yb_, ybk = tb_next()
                      cp('dve', yb_[:, 0:n], y[:, 0:n], [ykk], [ybk])
                      cst[(j, i)] = dict(o=o, n=n, y=y, ykk=ykk, yb_=yb_, ybk=ybk)

                  def c_s2(j, i):
                      c = cst[(j, i)]; n = c['n']; y = c['y']; ykk = c['ykk']
                      pmn, pmnk = ps_next()
                      mm(pmn[:, 0:n], [(CSTb[:, GAVG, :], c['yb_'][:, 0:n])], ['CSTb', c['ybk']], pmnk)
                      tt('dve', y[:, 0:n], y[:, 0:n], pmn[:, 0:n], ALU.subtract, [ykk, pmnk], [ykk])
                      sq_, sqk = tb_next()
                      act(sq_[:, 0:n], y[:, 0:n], AF.Square, [ykk], [sqk])
                      c['sq_'] = sq_; c['sqk'] = sqk

                  def c_s3(j, i):
                      c = cst[(j, i)]; o = c['o']; n = c['n']; y = c['y']; ykk = c['ykk']
                      pv, pvk = ps_next()
                      mm(pv[:, 0:n], [(CSTb[:, GAVG, :], c['sq_'][:, 0:n])], ['CSTb', c['sqk']], pvk)
                      r, rk = tmp_next()
                      rsqrt_into(r[:, 0:n], pv[:, 0:n], 1.0, [pvk], rk, n)
                      tt('dve', y[:, 0:n], y[:, 0:n], r[:, 0:n], ALU.mult, [ykk, rk], [ykk])
                      ts('dve', y[:, 0:n], y[:, 0:n], LNC[:, CLG + j:CLG + j + 1], LNC[:, CLB + j:CLB + j + 1], ALU.mult, ALU.add, [ykk, 'LNC'], [ykk])
                      act(Y[:, 2 + j, o:o + n], y[:, 0:n], AF.Silu, [ykk], [('Y', 2 + j, i)])

                  for q in range(len(ctiles) + 2):
                      if q < len(ctiles):
                          c_s1(*ctiles[q])
                      if 1 <= q <= len(ctiles):
                          c_s2(*ctiles[q - 1])
                      if q >= 2:
                          c_s3(*ctiles[q - 2])

                  chk('B', l, pq)
                  for half in range(2):
                      wC1, wC1k = load_w(w_in[l].rearrange("(k p) n -> p k n", p=128)[:, :, 1024 + half * 256:1280 + half * 256], KD, 256)
                      for i in range(len(SUBT)):
                          o, n = cur(i)
                          for c in range(2):
                              pst, pk, _ = proj(wC1, wC1k, i, c)
                              evac_z(pst, pk, n, (2 + c) if half == 0 else c, i)
                  wC2, wC2k = load_w(w_in[l].rearrange("(k p) n -> p k n", p=128)[:, :, 1536:1792], KD, 256)
                  for i in range(len(SUBT)):
                      o, n = cur(i)
                      for j in range(2):
                          pst, pk, _ = proj(wC2, wC2k, i, j)
                          zk_ = zkey(j, i)
                          tt('dve', Z[:, j, PAD + o:PAD + o + n], Z[:, j, PAD + o:PAD + o + n], pst[:, 0:n], ALU.mult, [zk_, pk], [zk_])
                  for i in range(len(SUBT)):
                      o, n = cur(i)
                      for j in range(2):
                          pc, pck = ps_next()
                          rd = [zkey(j, i), ('Zpad', j)] + ([zkey(j, i - 1)] if i > 0 else []) + [('DG3', q, j) for q in range(3)]
                          conv_diag(pc, n, pck, [DG3[:, q * 2 + j, :] for q in range(3)],
                                    (lambda j, o, n: lambda q: Z[:, j, PAD + o - 2 + q:PAD + o - 2 + q + n])(j, o, n), rd)
                          tt('dve', Y[:, 4 + j, o:o + n], Z[:, 2 + j, PAD + o:PAD + o + n], pc[:, 0:n], ALU.mult, [zkey(2 + j, i), pck], [('Y', 4 + j, i)])
                  chk('C', l, pq)
                  wD, wDk = load_w(w_in[l].rearrange("(k p) n -> p k n", p=128)[:, :, 1792:2048], KD, 256)
                  for i in range(len(SUBT)):
                      o, n = cur(i)
                      for j in range(2):
                          pst, pk, _ = proj(wD, wDk, i, j)
                          evac_z(pst, pk, n, j, i)
                  for i in range(len(SUBT)):
                      o, n = cur(i)
                      pls = []
                      for j in range(2):
                          ntap = 4 if j == 0 else 16
                          tb0 = 6 + (0 if j == 0 else 4)
                          pc, pck = ps_next()
                          rd = [zkey(j, i), ('Zpad', j), 'CSTb'] + ([zkey(j, i - 1)] if i > 0 else [])
                          conv_diag(pc, n, pck, [CSTb[:, tb0 + q, :] for q in range(ntap)],
                                    (lambda j, o, n: lambda q: Z[:, j, PAD + o - q:PAD + o - q + n])(j, o, n), rd)
                          plb, plbk = tb_next()
                          if pq == 0 and i == 0:
                              pl, plk = tmp_next()
                              cp('act', pl[:, 0:n], pc[:, 0:n], [pck], [plk])
                              tt('dve', pl[:, HALO - o:HALO - o + 16], pl[:, HALO - o:HALO - o + 16], CNT[:, j, :], ALU.mult, [plk, 'CNT'], [plk])
                              tt('dve', plb[:, 0:n], pl[:, 0:n], Z[:, j, PAD + o:PAD + o + n], ALU.subtract, [plk, zkey(j, i)], [plbk])
                          else:
                              tt('dve', plb[:, 0:n], pc[:, 0:n], Z[:, j, PAD + o:PAD + o + n], ALU.subtract, [pck, zkey(j, i)], [plbk])
                          pw, pwk = ps_next()
                          mm(pw[:, 0:n], [(POOLBD[:, j, :], plb[:, 0:n])], ['POOLBD', plbk], pwk)
                          act(Y[:, 6 + j, o:o + n], pw[:, 0:n], AF.Identity, [pwk, 'LNC'], [('Y', 6 + j, i)], scale=LNC[:, PSC + j:PSC + j + 1], bias=0.0)

                  chk('D', l, pq)
                  st0['v'] = 126 if l == 0 else 254
                  for dc in range(KD):
                      for kb in range(4):
                          c0 = 2048 + kb * 1024 + dc * 128
                          wg, wgk = load_w(w_in[l].rearrange("(k p) n -> p k n", p=128)[:, :, c0:c0 + 128], KD, 128)
                          wb_, wbk = load_w(w_branch[l, kb].rearrange("(j p) d -> p j d", p=128)[:, :, dc * 128:(dc + 1) * 128], 2, 128)
                          for i in range(len(SUBT)):
                              o, n = cur(i)
                              acc, acck = ACC[i], ('acc', i)
                              pg, pgk, _ = proj(wg, wgk, i, 0)
                              sg, sgk = tmp_next()
                              act(sg[:, 0:n], pg[:, 0:n], AF.Sigmoid, [pgk], [sgk])
                              pb, pbk = ps_next()
                              mm(pb[:, 0:n], [(wb_[:, j, :], Y[:, 2 * kb + j, o:o + n]) for j in range(2)],
                                 [wbk, ('Y', 2 * kb, i), ('Y', 2 * kb + 1, i)], pbk)
                              if kb == 0:
                                  tt('dve', acc[:, 0:n], pb[:, 0:n], sg[:, 0:n], ALU.mult, [pbk, sgk], [acck])
                              else:
                                  tt('dve', sg[:, 0:n], pb[:, 0:n], sg[:, 0:n], ALU.mult, [pbk, sgk], [sgk])
                                  if kb < 3:
                                      tt('dve', acc[:, 0:n], acc[:, 0:n], sg[:, 0:n], ALU.add, [acck, sgk], [acck])
                                  else:
                                      tt('dve', MG[:, dc, o:o + n], acc[:, 0:n], sg[:, 0:n], ALU.add, [acck, sgk], [('MG', dc, i)])
                  chk('merge', l, pq)
                  for dc in range(KD):
                      wo, wok = load_w(w_out[l].rearrange("(k p) n -> p k n", p=128)[:, :, dc * 128:(dc + 1) * 128], KD, 128)
                      for i in range(len(SUBT)):
                          o, n = cur(i)
                          pst, pk = ps_next()
                          mm(pst[:, 0:n], [(wo[:, k, :], MG[:, k, o:o + n]) for k in range(KD)], [wok] + [('MG', k, i) for k in range(KD)], pk)
                          tt('dve', X[:, dc, o:o + n], X[:, dc, o:o + n], pst[:, 0:n], ALU.add, [xk(dc, i), pk], [xk(dc, i)])

                  chk('xmid', l, pq)
                  rmsnorm((l * 3 + 1) * KD, 'H', True, pq)
                  for (j0, nj) in JGROUPS:
                      for jj in range(nj):
                          j = j0 + jj
                          wg, wgk = load_w(w_up[l].rearrange("(k p) n -> p k n", p=128)[:, :, j * 128:(j + 1) * 128], KD, 128)
                          wv, wvk = load_w(w_up[l].rearrange("(k p) n -> p k n", p=128)[:, :, DFF + j * 128:DFF + (j + 1) * 128], KD, 128)
                          for i in range(len(SUBT)):
                              o, n = cur(i)
                              res = []
                              for (U, un, cj, ww, wwk) in ((UG, 'UG', j, wg, wgk), (UV, 'UV', NJ + j, wv, wvk)):
                                  pst, pk, _ = proj(ww, wwk, i, 0)
                                  cp('act', U[:, PAD + o:PAD + o + n], pst[:, 0:n], [pk], [(un, i)])
                                  a, ak = tmp_next()
                                  wbase = l * 132 + cj
                                  rd = [(un, i), un + 'pad', 'FWC', 'FBC'] + ([(un, i - 1)] if i > 0 else [])
                                  if un == 'UG':
                                      act(a[:, 0:n], pst[:, 0:n], AF.Identity, [pk, 'FWC', 'FBC'], [ak],
                                          scale=FWC[:, wbase + 88:wbase + 89], bias=FBC[:, l * 44 + cj:l * 44 + cj + 1])
                                  else:
                                      ts('pool', a[:, 0:n], U[:, PAD + o:PAD + o + n], FWC[:, wbase + 88:wbase + 89], FBC[:, l * 44 + cj:l * 44 + cj + 1],
                                         ALU.mult, ALU.add, rd, [ak])
                                  stt('dve', a[:, 0:n], U[:, PAD + o - 1:PAD + o - 1 + n], FWC[:, wbase + 44:wbase + 45], a[:, 0:n], ALU.mult, ALU.add, rd + [ak], [ak])
                                  stt('dve', a[:, 0:n], U[:, PAD + o - 2:PAD + o - 2 + n], FWC[:, wbase:wbase + 1], a[:, 0:n], ALU.mult, ALU.add, rd + [ak], [ak])
                                  res.append((a, ak))
                              (ga, gak), (va, vak) = res
                              sg, sgk = tmp_next()
                              act(sg[:, 0:n], ga[:, 0:n], AF.Silu, [gak], [sgk])
                              tt('dve', Y[:, jj, o:o + n], sg[:, 0:n], va[:, 0:n], ALU.mult, [sgk, vak], [('Y', jj, i)])
                      for dc in range(KD):
                          wd, wdk = load_w(w_down[l].rearrange("(k p) n -> p k n", p=128)[:, j0:j0 + nj, dc * 128:(dc + 1) * 128], nj, 128)
                          for i in range(len(SUBT)):
                              o, n = cur(i)
                              pst, pk = ps_next()
                              mm(pst[:, 0:n], [(wd[:, jj, :], Y[:, jj, o:o + n]) for jj in range(nj)], [wdk] + [('Y', jj, i) for jj in range(nj)], pk)
                              tt('dve', X[:, dc, o:o + n], X[:, dc, o:o + n], pst[:, 0:n], ALU.add, [xk(dc, i), pk], [xk(dc, i)])

                  chk('ffn', l, pq)
                  rmsnorm((l * 3 + 2) * KD, 'H', False)
                  ptkeys = [zkey(jz, iz) for jz in range(2) for iz in range(len(SUBT))]
                  cdma(Z[:, 0:2, PAD:PAD + T], pT[pq, l].rearrange("(j p) t -> p j t", p=128), ptkeys, 'PT')
                  for dc in range(KD):
                      wg, wgk = load_w(w_pg[l].rearrange("(k p) n -> p k n", p=128)[:, :, dc * 128:(dc + 1) * 128], KD, 128)
                      wp, wpk = load_w(w_pp[l].rearrange("(j p) n -> p j n", p=128)[:, :, dc * 128:(dc + 1) * 128], 2, 128)
                      for i in range(len(SUBT)):
                          o, n = cur(i)
                          pg, pgk, _ = proj(wg, wgk, i, 0)
                          sg, sgk = tmp_next()
                          act(sg[:, 0:n], pg[:, 0:n], AF.Sigmoid, [pgk], [sgk])
                          pp, ppk = ps_next()
                          mm(pp[:, 0:n], [(wp[:, j, :], Z[:, j, PAD + o:PAD + o + n]) for j in range(2)], [wpk, zkey(0, i), zkey(1, i)], ppk)
                          tt('dve', sg[:, 0:n], pp[:, 0:n], sg[:, 0:n], ALU.mult, [ppk, sgk], [sgk])
                          tt('dve', X[:, dc, o:o + n], X[:, dc, o:o + n], sg[:, 0:n], ALU.add, [xk(dc, i), sgk], [xk(dc, i)])

              chk('end', L - 1, pq)
              st0['v'] = 0
              fin_sub = [(256, 256, 0), (512, 512, 1), (1024, 256, 2)]
              for (o, n, i) in fin_sub:
                  pst, pk = ps_next()
                  for k in range(KD):
                      tb, tk = tb_next()
                      act(tb[:, 0:n], X[:, k, o:o + n], AF.Square, [xk(k, i)], [tk])
                      S.add('pe', (lambda tb, k, pst, o, n: lambda e: e.matmul(pst[:, 0:n], lhsT=CSTb[:, ONES, :], rhs=tb[:, 0:n], start=(k == 0), stop=(k == KD - 1)))(tb, k, pst, o, n),
                            reads=[tk, 'CSTb'], writes=[pk])
                  r, rk = tmp_next()
                  rsqrt_into(r[:, 0:n], pst[:, 0:n], 1.0 / D, [pk], rk, n)
                  for k in range(KD):
                      ot, otk = tmp_next()
                      stt('dve', ot[:, 0:n], X[:, k, o:o + n], GC[:, L * 3 * KD + k:L * 3 * KD + k + 1], r[:, 0:n], ALU.mult, ALU.mult, [xk(k, i), rk, 'GC'], [otk])
                      okey = ('out', pq, k, o)
                      S.add('sp', (lambda ot, k, o, n, pq: lambda e: e.dma_start(out=outT[pq, k * 128:(k + 1) * 128, o - 256:o - 256 + n], in_=ot[:, 0:n]))(ot, k, o, n, pq),
                            reads=[otk], writes=[okey], dma_key=('o', k))
                      outs.append(S.ops[-1])

        except _Stop:
            pass
        if debug is not None:
            allkeys = list(S.last_writer.keys())
            for nm, (dap, tns) in dbg.items():
                S.add('sp', (lambda dap, tns: lambda e: e.dma_start(out=dap[:], in_=tns[:]))(dap, tns), reads=allkeys, writes=[('dbg', nm)], dma_key=('dbg', nm))
                outs.append(S.ops[-1])
        S.finalize(nc, st)
        with nc.Block() as block0:
            @block0.gpsimd
            def _(e):
                for sem in S.sems.values():
                    e.sem_clear(sem)
                for sem in S.sems.values():
                    e.wait_op(sem, 0, "sem-eq")
        with nc.Block() as block:
            @block.sync
            def _(e):
                fw = {}
                for op in outs:
                    fw[op.token[0]] = max(fw.get(op.token[0], 0), op.token[1])
                S.emit_engine('sp', e, list(fw.items()))

            @block.scalar
            def _(e): S.emit_engine('act', e)

            @block.vector
            def _(e): S.emit_engine('dve', e)

            @block.gpsimd
            def _(e): S.emit_engine('pool', e)

            @block.tensor
            def _(e): S.emit_engine('pe', e)
    return nc


def _cols(v):
    return np.ascontiguousarray(np.asarray(v, np.float32).reshape(-1, 128).T)


def prepare(x, p, g_mix, w_in, gmlp_ln_g, gmlp_ln_b, gmlp_w_s, gmlp_b_s,
           conf_w_dw, conf_b_dw, conf_ln_g, conf_ln_b, short_w, pool_w, pool_scale,
           w_branch, w_out, g_ffn, ffn_w_up, ffn_w_conv, ffn_b_conv, ffn_w_down,
           g_ple, ple_w_gate, ple_w_proj, g_final):
    f = lambda a: np.ascontiguousarray(np.asarray(a, np.float32))
    x = f(x); p = f(p)
    B, SEQ, _ = x.shape
    gcols = np.concatenate([_cols(v[l]) for l in range(L) for v in (g_mix, g_ffn, g_ple)] + [_cols(g_final)], axis=1)
    lncols = np.concatenate([_cols(np.asarray(v)[l].reshape(-1)) for l in range(L)
                             for v in (gmlp_ln_g, gmlp_ln_b, conf_b_dw, conf_ln_g, conf_ln_b, pool_scale)], axis=1)
    shortw = np.concatenate([_cols(np.asarray(short_w)[l, q]) for l in range(L) for q in range(3)], axis=1)
    confw = np.concatenate([_cols(np.asarray(conf_w_dw)[l, q]) for l in range(L) for q in range(31)], axis=1)
    ffnwc = np.concatenate([_cols(np.asarray(ffn_w_conv)[l, q]) for l in range(L) for q in range(3)], axis=1)
    ffnbc = np.concatenate([_cols(np.asarray(ffn_b_conv)[l]) for l in range(L)], axis=1)
    gwT = f(np.transpose(np.asarray(gmlp_w_s), (0, 1, 3, 2)))
    gbs = f(np.asarray(gmlp_b_s).reshape(1, -1))
    pw = np.asarray(pool_w, np.float32)
    poolbd = np.zeros((L, 2, 128, 128), np.float32)
    for l in range(L):
        for j in range(2):
            poolbd[l, j, :64, :64] = pw[l, 2 * j]
            poolbd[l, j, 64:, 64:] = pw[l, 2 * j + 1]
    consts = np.zeros((26, 128, 128), np.float32)
    consts[0] = np.eye(128)
    consts[1] = np.triu(np.ones((128, 128)))
    consts[2][:, :64] = 1.0
    consts[3][:, 64:] = 1.0
    consts[4][:64, :64] = 1.0 / 64; consts[4][64:, 64:] = 1.0 / 64
    consts[5] = 1.0
    pidx = np.arange(128)
    for j in range(2):
        ntap = 4 if j == 0 else 16
        tb0 = 6 + (0 if j == 0 else 4)
        win = np.where(pidx < 64, WINS[2 * j], WINS[2 * j + 1]).astype(np.float32)
        for q in range(ntap):
            consts[tb0 + q][pidx, pidx] = np.where(q < win, 1.0 / win, 0.0)
    in_maps = []
    for c in range(8):
        b, seg = c // 4, c % 4
        xT = np.zeros((NPASS, D, T), np.float32)
        pT = np.zeros((NPASS, L, 256, T), np.float32)
        mask = np.full((128, HALO), 0.0 if seg == 0 else 1.0, np.float32)
        cnt = np.ones((128, 2, 16), np.float32)
        tpos16 = seg * 2048 + np.arange(16)
        for j in range(2):
            win = np.where(pidx < 64, WINS[2 * j], WINS[2 * j + 1]).astype(np.float32)[:, None]
            cnt[:, j, :] = win / np.minimum(tpos16[None, :] + 1, win)
        for q in range(NPASS):
            s = seg * 2048 + q * TOUT - HALO
            lo = max(s, 0)
            xT[q, :, lo - s:] = x[b, lo:s + T].T
            for l in range(L):
                pT[q, l, :, lo - s:] = p[l, b, lo:s + T].T
        in_maps.append({
            "xT": xT, "pT": pT, "mask": mask, "cnt": cnt,
            "w_in": f(w_in), "w_branch": f(w_branch), "w_out": f(w_out), "w_up": f(ffn_w_up), "w_down": f(ffn_w_down),
            "w_pg": f(ple_w_gate), "w_pp": f(ple_w_proj), "gcols": f(gcols), "lncols": f(lncols), "shortw": f(shortw),
            "confw": f(confw), "ffnwc": f(ffnwc), "ffnbc": f(ffnbc), "gwT": gwT, "gbs": gbs, "poolbd": poolbd, "consts": consts,
        })
    return in_maps


def kernel(**inputs):
    in_maps = prepare(**inputs)
    B, SEQ = 2, 8192
    nc = build_nc()
    res = run_bass_kernel_spmd(nc, in_maps, core_ids=list(range(8)))
    out = np.zeros((B, SEQ, D), np.float32)
    for c in range(8):
        b, seg = c // 4, c % 4
        o = res.results[c]["outT"]
        for q in range(NPASS):
            s = seg * 2048 + q * TOUT
            out[b, s:s + TOUT, :] = o[q].T
    return out
```

```python
from contextlib import ExitStack
import numpy as np
import concourse.bass as bass
import concourse.mybir as mybir
from concourse.bass_utils import run_bass_kernel_spmd

F32 = mybir.dt.float32
BF16 = mybir.dt.bfloat16
AF = mybir.ActivationFunctionType
ALU = mybir.AluOpType
AX = mybir.AxisListType

L = 2
D = 1024
KD = 8
HALO = 256
TOUT = 1024
T = HALO + TOUT
PAD = 32
NPASS = 2
SUBT = [(0, 512), (512, 512), (1024, 256)]
DFF = 2816
NJ = 22
JGROUPS = [(0, 8), (8, 8), (16, 6)]
EPS = 1e-6
WINS = (2, 4, 8, 16)


class _Op:
    __slots__ = ('eng', 'emit', 'is_dma', 'dma_key', 'deps', 'signal', 'token', 'waits', 'idx', 'snapshot', 'stage', 'hoist')


class Sched:
    def __init__(self):
        self.ops = []
        self.last_writer = {}
        self.readers = {}
        self.stage = ''

    def add(self, eng, emit, reads=(), writes=(), dma_key=None, hoist=False):
        op = _Op()
        op.hoist = hoist
        op.eng = eng
        op.emit = emit
        op.is_dma = dma_key is not None
        op.dma_key = dma_key
        op.signal = op.is_dma
        op.token = None
        op.idx = len(self.ops)
        op.stage = self.stage
        deps = {}
        for k in reads:
            w = self.last_writer.get(k)
            if w is not None:
                deps[w.idx] = (w, True)
        for k in writes:
            w = self.last_writer.get(k)
            if w is not None:
                deps[w.idx] = (w, True)
            for r in self.readers.get(k, ()):
                if r.idx not in deps:
                    deps[r.idx] = (r, False)
        for k in reads:
            self.readers.setdefault(k, []).append(op)
        for k in writes:
            self.last_writer[k] = op
            self.readers[k] = []
        op.deps = []
        for i in sorted(deps):
            d, hard = deps[i]
            if not d.is_dma and not op.is_dma and d.eng == op.eng:
                if op.eng == 'pe' or not hard:
                    continue
            d.signal = True
            op.deps.append(d)
        self.ops.append(op)
        return op

    def finalize(self, nc, stack):
        keyed = []
        for op in self.ops:
            if op.hoist:
                m = max([d.idx for d in op.deps], default=-1)
                keyed.append(((m, 1, op.idx), op))
            else:
                keyed.append(((op.idx, 0, 0), op))
        keyed.sort(key=lambda kv: kv[0])
        self.ops = [op for _, op in keyed]
        self.sems = {}
        cnt = {}
        for op in self.ops:
            if not op.signal:
                continue
            key = ('dma', op.dma_key) if op.is_dma else ('eng', op.eng)
            if key not in self.sems:
                self.sems[key] = stack.enter_context(nc.semaphore("s%d" % len(self.sems)))
                cnt[key] = 0
            cnt[key] += 16 if op.is_dma else 1
            op.token = (key, cnt[key])
        known = {}
        for op in self.ops:
            K = known.setdefault(op.eng, {})
            waits = {}
            for d in op.deps:
                key, val = d.token
                if K.get(key, 0) >= val:
                    continue
                waits[key] = max(waits.get(key, 0), val)
                for s, v in d.snapshot.items():
                    if K.get(s, 0) < v:
                        K[s] = v
                K[key] = val
            op.waits = list(waits.items())
            op.snapshot = dict(K)

    def emit_engine(self, name, eng, final_waits=()):
        for op in self.ops:
            if op.eng != name:
                continue
            for key, val in op.waits:
                eng.wait_ge(self.sems[key], val)
            ins = op.emit(eng)
            if op.signal:
                ins.then_inc(self.sems[op.token[0]], 16 if op.is_dma else 1)
        for key, val in final_waits:
            eng.wait_ge(self.sems[key], val)


class _Stop(Exception):
    pass


def build_nc(debug=None):
    nc = bass.Bass("TRN2", target_bir_lowering=False)

    def din(name, shape):
        return nc.dram_tensor(name, list(shape), F32, kind="ExternalInput").ap()

    xT = din("xT", [NPASS, D, T])
    pT = din("pT", [NPASS, L, 256, T])
    maskd = din("mask", [128, HALO])
    cntd = din("cnt", [128, 2, 16])
    w_in = din("w_in", [L, D, 6144])
    w_branch = din("w_branch", [L, 4, 256, D])
    w_out = din("w_out", [L, D, D])
    w_up = din("w_up", [L, D, 2 * DFF])
    w_down = din("w_down", [L, DFF, D])
    w_pg = din("w_pg", [L, D, D])
    w_pp = din("w_pp", [L, 256, D])
    gcols_d = din("gcols", [128, L * 3 * KD + KD])
    lncols_d = din("lncols", [128, L * 6 * 2])
    shortw_d = din("shortw", [128, L * 3 * 2])
    confw_d = din("confw", [128, L * 31 * 2])
    ffnwc_d = din("ffnwc", [128, L * 3 * 44])
    ffnbc_d = din("ffnbc", [128, L * 44])
    gwT_d = din("gwT", [L, 4, 128, 128])
    gbs_d = din("gbs", [1, L * 4 * 128])
    poolbd_d = din("poolbd", [L, 2, 128, 128])
    consts_d = din("consts", [6 + 20, 128, 128])
    outT = nc.dram_tensor("outT", [NPASS, D, TOUT], F32, kind="ExternalOutput").ap()

    S = Sched()
    st = ExitStack()
    with st:
        def sb(name, shape, dt=F32):
            return st.enter_context(nc.sbuf_tensor(name, list(shape), dt))

        X = sb("X", [128, KD, T])
        H = sb("H", [128, KD, T], BF16)
        Z = sb("Z", [128, 4, PAD + T], BF16)
        Y = sb("Y", [128, 8, T], BF16)
        MG = sb("MG", [128, KD, T], BF16)
        UG = sb("UG", [128, PAD + T])
        UV = sb("UV", [128, PAD + T])
        MASK = sb("MASK", [128, HALO])
        CNT = sb("CNT", [128, 2, 16])
        ZN2 = sb("ZN2", [128, 12, 128], BF16)
        BIAS2 = sb("BIAS2", [128, 2, 128])
        GWT = sb("GWT", [128, 4, 128], BF16)
        GWTs = sb("GWTs", [128, 4, 128], BF16)
        GBSb = sb("GBSb", [1, L * 4 * 128], BF16)
        POOLBD = sb("POOLBD", [128, 2, 128], BF16)
        CST = sb("CST", [128, 2, 128])
        CSTb = sb("CSTb", [128, 26, 128], BF16)
        DG31 = sb("DG31", [128, 62, 128], BF16)
        DG3 = sb("DG3", [128, 6, 128], BF16)
        GC = sb("GC", [128, L * 3 * KD + KD])
        LNC = sb("LNC", [128, L * 12])
        SHW = sb("SHW", [128, L * 6])
        CFW = sb("CFW", [128, L * 62])
        FWC = sb("FWC", [128, L * 132])
        FBC = sb("FBC", [128, L * 44])
        EPSC = sb("EPSC", [128, 1])
        NT = 8
        TMP = [sb("TMP%d" % i, [128, 512]) for i in range(NT)]
        TB = [sb("TB%d" % i, [128, 512], BF16) for i in range(4)]
        COL = sb("COL", [128, 8])
        COLS = sb("COLS", [128, 16])
        COLQ = sb("COLQ", [128, 16])
        COLM = sb("COLM", [128, 16])
        NWB = 5
        WB = [sb("WB%d" % i, [128, 2048], BF16) for i in range(NWB)]
        ACC = [sb("ACC%d" % i, [128, 512]) for i in range(len(SUBT))]
        PS = [st.enter_context(nc.psum_tensor("PS%d" % i, [128, 512], F32)) for i in range(8)]

        IDENT, TRI, SELA, SELB, GAVG, ONES = 0, 1, 2, 3, 4, 5
        state = {'ps': 0, 'ws': 0, 'wb': 0, 'tmp': 0, 'tb': 0, 'dmaq': 0}

        def ps_next():
            i = state['ps']; state['ps'] = (i + 1) % 8
            return PS[i], ('ps', i)

        def tmp_next():
            i = state['tmp']; state['tmp'] = (i + 1) % NT
            return TMP[i], ('tmp', i)

        def tb_next():
            i = state['tb']; state['tb'] = (i + 1) % 4
            return TB[i], ('tb', i)

        def dma(out_ap, in_ap, wkey, reads=()):
            S.add('sp', lambda e: e.dma_start(out=out_ap, in_=in_ap), reads=list(reads), writes=[wkey], dma_key=wkey)

        def load_w(src3, kc, ncol):
            j = state['wb']; state['wb'] = (j + 1) % NWB
            n = kc * ncol
            bview = WB[j][:, 0:n].rearrange("p (k n) -> p k n", k=kc)
            cdma(bview, src3, [('wb', j)], ('wb', j), hoist=True)
            return bview, ('wb', j)

        def cdma(out_ap, in_ap, wkeys, dkey, hoist=False):
            S.add('pool', lambda e: e.dma_start(out=out_ap, in_=in_ap), writes=list(wkeys), dma_key=dkey, hoist=hoist)

        def mm(out_ap, pairs, reads, wkey):
            def f(e):
                n = len(pairs)
                for q, (l, r) in enumerate(pairs):
                    ins = e.matmul(out_ap, lhsT=l, rhs=r, start=(q == 0), stop=(q == n - 1))
                return ins
            S.add('pe', f, reads=list(reads), writes=[wkey])

        def act(out, in_, func, reads, writes, **kw):
            S.add('act', lambda e: e.activation(out=out, in_=in_, func=func, **kw), reads=list(reads), writes=list(writes))

        def tt(eng, out, a, b, op, reads, writes):
            S.add(eng, lambda e: e.tensor_tensor(out=out, in0=a, in1=b, op=op), reads=list(reads), writes=list(writes))

        def ts(eng, out, a, s1, s2, op0, op1, reads, writes):
            if s2 is None:
                S.add(eng, lambda e: e.tensor_scalar(out=out, in0=a, scalar1=s1, scalar2=None, op0=op0), reads=list(reads), writes=list(writes))
            else:
                S.add(eng, lambda e: e.tensor_scalar(out=out, in0=a, scalar1=s1, scalar2=s2, op0=op0, op1=op1), reads=list(reads), writes=list(writes))

        def stt(eng, out, a, sc, b, op0, op1, reads, writes):
            S.add(eng, lambda e: e.scalar_tensor_tensor(out=out, in0=a, scalar=sc, in1=b, op0=op0, op1=op1), reads=list(reads), writes=list(writes))

        def cp(eng, out, in_, reads, writes):
            if eng == 'act':
                S.add(eng, lambda e: e.activation(out=out, in_=in_, func=AF.Copy), reads=list(reads), writes=list(writes))
            else:
                S.add(eng, lambda e: e.tensor_copy(out=out, in_=in_), reads=list(reads), writes=list(writes))

        def rsqrt_into(out, in_ap, scale, reads, wkey, n):
            ts('dve', out, in_ap, scale, EPS, ALU.mult, ALU.add, reads, [wkey])
            act(out, out, AF.Sqrt, [wkey], [wkey])
            S.add('dve', lambda e: e.reciprocal(out=out, in_=out), reads=[wkey], writes=[wkey])

        dma(CST[:], consts_d[0:2].rearrange("c p n -> p c n"), 'CST')
        cdma(CSTb[:], consts_d.rearrange("c p n -> p c n"), ['CSTb'], 'CSTb')
        dma(GC[:], gcols_d, 'GC'); dma(LNC[:], lncols_d, 'LNC'); dma(SHW[:], shortw_d, 'SHW')
        dma(CFW[:], confw_d, 'CFW'); dma(FWC[:], ffnwc_d, 'FWC'); dma(FBC[:], ffnbc_d, 'FBC')
        cdma(GBSb[:], gbs_d, ['GBSb'], 'GBSb')
        dma(MASK[:], maskd, 'MASK')
        dma(CNT[:], cntd, 'CNT')
        S.add('pool', lambda e: e.memset(EPSC[:], EPS), writes=['EPSC'])
        S.add('pool', lambda e: e.memset(ZN2[:], 0.0), writes=['ZN2'])
        for zi in range(4):
            S.add('pool', (lambda zi: lambda e: e.memset(Z[:, zi, 0:PAD], 0.0))(zi), writes=[('Zpad', zi)])
        S.add('pool', lambda e: e.memset(UG[:, 0:PAD], 0.0), writes=['UGpad'])
        S.add('pool', lambda e: e.memset(UV[:, 0:PAD], 0.0), writes=['UVpad'])

        def xk(k, i): return ('X', k, i)

        st0 = {'v': 0}

        def cur(i):
            o, n = SUBT[i]
            if i == 0:
                return st0['v'], n - st0['v']
            return o, n

        def rmsnorm(gbase, hk, masked, pq=0):
            for i in range(len(SUBT)):
                o, n = cur(i)
                pst, pk = ps_next()
                tbs = []
                for k in range(KD):
                    tb, tk = tb_next()
                    act(tb[:, 0:n], X[:, k, o:o + n], AF.Square, [xk(k, i)], [tk])
                    tbs.append((tb, tk))
                    if len(tbs) == 4 or k == KD - 1:
                        k0 = k - len(tbs) + 1
                        for q, (tb2, tk2) in enumerate(tbs):
                            kk = k0 + q
                            S.add('pe', (lambda tb2, kk, pst, n: lambda e: e.matmul(pst[:, 0:n], lhsT=CSTb[:, ONES, :], rhs=tb2[:, 0:n], start=(kk == 0), stop=(kk == KD - 1)))(tb2, kk, pst, n),
                                  reads=[tk2, 'CSTb'], writes=[pk])
                        tbs = []
                r, rk = tmp_next()
                rsqrt_into(r[:, 0:n], pst[:, 0:n], 1.0 / D, [pk], rk, n)
                if masked and pq == 0 and i == 0:
                    tt('dve', r[:, 0:HALO - o], r[:, 0:HALO - o], MASK[:, o:HALO], ALU.mult, [rk, 'MASK'], [rk])
                for k in range(KD):
                    stt('dve', H[:, k, o:o + n], X[:, k, o:o + n], GC[:, gbase + k:gbase + k + 1], r[:, 0:n], ALU.mult, ALU.mult,
                        [xk(k, i), rk, 'GC'], [(hk, k, i)])

        def hreads(i): return [('H', k, i) for k in range(KD)]

        def proj(wcols, wk, i, c):
            o, n = cur(i)
            pst, pk = ps_next()
            mm(pst[:, 0:n], [(wcols[:, k, c * 128:(c + 1) * 128], H[:, k, o:o + n]) for k in range(KD)], hreads(i) + [wk], pk)
            return pst, pk, n

        def conv_diag(pst, n, pk, mats, src_fn, reads):
            mm(pst[:, 0:n], [(mats[q], src_fn(q)) for q in range(len(mats))], reads, pk)

        outs = []

        def chk(tag, l, pq):
            S.stage = 'after_%s_L%d_P%d' % (tag, l, pq)
            if debug is not None and debug == (tag, l) and pq == 0:
                raise _Stop()

        dbg = {}
        if debug is not None:
            for nm, tns, dt in (("dX", X, F32), ("dH", H, BF16), ("dY", Y, BF16), ("dMG", MG, BF16), ("dZ", Z, BF16), ("dUG", UG, F32)):
                dbg[nm] = (nc.dram_tensor(nm, list(tns.shape), dt, kind="ExternalOutput").ap(), tns)
        try:
          for pq in range(NPASS):
              for k in range(KD):
                  for i, (o, n) in enumerate(SUBT):
                      dma(X[:, k, o:o + n], xT[pq, k * 128:(k + 1) * 128, o:o + n], xk(k, i))
              for l in range(L):
                  lc = l * 12
                  LNG, LNB, CBD, CLG, CLB, PSC = [lc + 2 * q for q in range(6)]
                  cdma(GWTs[:], gwT_d[l].rearrange("g s t -> s g t"), ['GWTs'], 'GWTs')
                  for g in range(4):
                      tt('dve', GWT[:, g, :], GWTs[:, g, :], CSTb[:, TRI, :], ALU.mult, ['GWTs', 'CSTb'], [('GWT', g)])
                  cdma(POOLBD[:], poolbd_d[l].rearrange("j c d -> c j d"), ['POOLBD'], 'POOLBD')
                  for q in range(3):
                      for j in range(2):
                          ts('dve', DG3[:, q * 2 + j, :], CST[:, IDENT, :], SHW[:, l * 6 + q * 2 + j:l * 6 + q * 2 + j + 1], None, ALU.mult, None,
                             ['CST', 'SHW'], [('DG3', q, j)])
                  for j in range(2):
                      p1, k1 = ps_next()
                      mm(p1[:, 0:128], [(CSTb[:, SELA, :], GWT[:, 2 * j, :]), (CSTb[:, SELB, :], GWT[:, 2 * j + 1, :])],
                         ['CSTb', ('GWT', 2 * j), ('GWT', 2 * j + 1)], k1)
                      p2, k2 = ps_next()
                      b0 = (l * 4 + 2 * j) * 128
                      mm(p2[:, 0:128], [(CSTb[0:1, SELA, :], GBSb[0:1, b0:b0 + 128]), (CSTb[0:1, SELB, :], GBSb[0:1, b0 + 128:b0 + 256])],
                         ['CSTb', 'GBSb'], k2)
                      t2, tk2 = tmp_next()
                      cp('act', t2[:, 0:128], p2[:, 0:128], [k2], [tk2])
                      stt('dve', BIAS2[:, j, :], p1[:, 0:128], LNC[:, LNB + j:LNB + j + 1], t2[:, 0:128], ALU.mult, ALU.add,
                          [k1, tk2, 'LNC'], [('BIAS2', j)])

                  st0['v'] = 0 if l == 0 else 128
                  rmsnorm((l * 3 + 0) * KD, 'H', True, pq)
                  for j in range(2):
                      for q in range(31):
                          ts('dve', DG31[:, j * 31 + q, :], CST[:, IDENT, :], CFW[:, l * 62 + q * 2 + j:l * 62 + q * 2 + j + 1], None, ALU.mult, None,
                             ['CST', 'CFW'], [('DG31', j, q)])

                  def zkey(zi, i): return ('Z', zi, i)

                  def evac_z(pst, pk, n, zi, i):
                      o = cur(i)[0]
                      cp('act', Z[:, zi, PAD + o:PAD + o + n], pst[:, 0:n], [pk], [zkey(zi, i)])

                  def gelu_from_psum(pst, pk, n, out_ap, wkeys):
                      act(out_ap, pst[:, 0:n], AF.Gelu_apprx_tanh, [pk], wkeys)

                  chk('h', l, pq)
                  wA, wAk = load_w(w_in[l].rearrange("(k p) n -> p k n", p=128)[:, :, 0:256], KD, 256)
                  wV, wVk = load_w(w_in[l].rearrange("(k p) n -> p k n", p=128)[:, :, 256:512], KD, 256)
                  st0['v'] = 126 if l == 0 else 254
                  for i in range(len(SUBT)):
                      o, n = cur(i)
                      for j in range(2):
                          pst, pk, _ = proj(wA, wAk, i, j)
                          gelu_from_psum(pst, pk, n, Y[:, j, o:o + n], [('Y', j, i)])
                  st0['v'] = 0 if l == 0 else 128
                  def gzb(ch):
                      return Z[:, ch // 5, PAD + (ch % 5) * 256:PAD + (ch % 5) * 256 + 256]

                  def gzkeys(ch):
                      return [zkey(ch // 5, iz) for iz in range(len(SUBT))]

                  def v_front(ch):
                      t0 = ch * 128
                      i = t0 // 512
                      pst, pk = ps_next()
                      mm(pst[:, 0:256], [(H[:, k, t0:t0 + 128], wV[:, k, 0:256]) for k in range(KD)], hreads(i) + [wVk], pk)
                      gz, gk = tmp_next()
                      gelu_from_psum(pst, pk, 256, gz[:, 0:256], [gk])
                      sq, sk = tmp_next()
                      act(sq[:, 0:256], gz[:, 0:256], AF.Square, [gk], [sk])
                      S.add('dve', (lambda gz, ch: lambda e: e.reduce_sum(out=COLS[:, ch:ch + 1], in_=gz[:, 0:256], axis=AX.X))(gz, ch), reads=[gk], writes=[('COLS', ch)])
                      S.add('dve', (lambda sq, ch: lambda e: e.reduce_sum(out=COLQ[:, ch:ch + 1], in_=sq[:, 0:256], axis=AX.X))(sq, ch), reads=[sk], writes=[('COLQ', ch)])
                      cp('pool', gzb(ch), gz[:, 0:256], [gk], gzkeys(ch) + [('GZB', ch)])

                  def v_stats(chs):
                      c0, c1 = chs[0], chs[-1] + 1
                      rs = [('COLS', c) for c in chs]; rq = [('COLQ', c) for c in chs]
                      ts('dve', COLS[:, c0:c1], COLS[:, c0:c1], 1.0 / 256, None, ALU.mult, None, rs, ['MEAN'])
                      tt('dve', COLM[:, c0:c1], COLS[:, c0:c1], COLS[:, c0:c1], ALU.mult, ['MEAN'], ['M2'])
                      stt('dve', COLQ[:, c0:c1], COLQ[:, c0:c1], 1.0 / 256, COLM[:, c0:c1], ALU.mult, ALU.subtract, rq + ['M2'], ['RSTD'])
                      ts('dve', COLQ[:, c0:c1], COLQ[:, c0:c1], 1.0, EPS, ALU.mult, ALU.add, ['RSTD'], ['RSTD'])
                      act(COLQ[:, c0:c1], COLQ[:, c0:c1], AF.Sqrt, ['RSTD'], ['RSTD'])
                      S.add('dve', lambda e: e.reciprocal(out=COLQ[:, c0:c1], in_=COLQ[:, c0:c1]), reads=['RSTD'], writes=['RSTD'])

                  def v_back(ch):
                      t0 = ch * 128
                      i = t0 // 512
                      zb = ch % 3
                      for g in range(4):
                          ts('dve', ZN2[:, zb * 4 + g, (g % 2) * 64:(g % 2) * 64 + 64], gzb(ch)[:, g * 64:(g + 1) * 64], COLS[:, ch:ch + 1], COLQ[:, ch:ch + 1],
                             ALU.subtract, ALU.mult, gzkeys(ch) + [('GZB', ch), 'MEAN', 'RSTD', 'ZN2'], [('ZN2', zb, g)])
                      for j in range(2):
                          pm, pmk = ps_next()
                          mm(pm[:, 0:128], [(ZN2[:, zb * 4 + 2 * j, :], GWT[:, 2 * j, :]), (ZN2[:, zb * 4 + 2 * j + 1, :], GWT[:, 2 * j + 1, :])],
                             [('ZN2', zb, 2 * j), ('ZN2', zb, 2 * j + 1), ('GWT', 2 * j), ('GWT', 2 * j + 1)], pmk)
                          m, mk = tmp_next()
                          stt('dve', m[:, 0:128], pm[:, 0:128], LNC[:, LNG + j:LNG + j + 1], BIAS2[:, j, :], ALU.mult, ALU.add,
                              [pmk, 'LNC', ('BIAS2', j)], [mk])
                          yk = ('Y', j, i)
                          tt('dve', Y[:, j, t0:t0 + 128], Y[:, j, t0:t0 + 128], m[:, 0:128], ALU.mult, [yk, mk], [yk])

                  chs = list(range(st0['v'] // 128, T // 128))
                  for ch in chs:
                      v_front(ch)
                  v_stats(chs)
                  for ch in chs:
                      v_back(ch)

                  chk('A', l, pq)
                  st0['v'] = 96 if l == 0 else 224
                  wB, wBk = load_w(w_in[l].rearrange("(k p) n -> p k n", p=128)[:, :, 512:768], KD, 256)
                  wBg, wBgk = load_w(w_in[l].rearrange("(k p) n -> p k n", p=128)[:, :, 768:1024], KD, 256)
                  for i in range(len(SUBT)):
                      o, n = cur(i)
                      for j in range(2):
                          pg, pgk, _ = proj(wBg, wBgk, i, j)
                          sg, sgk = tmp_next()
                          act(sg[:, 0:n], pg[:, 0:n], AF.Sigmoid, [pgk], [sgk])
                          pa, pak, _ = proj(wB, wBk, i, j)
                          tt('dve', Z[:, j, PAD + o:PAD + o + n], pa[:, 0:n], sg[:, 0:n], ALU.mult, [pak, sgk], [zkey(j, i)])
                  ctiles = [(j, i) for j in range(2) for i in range(len(SUBT))]
                  cst = {}

                  def c_s1(j, i):
                      o, n = cur(i)
                      pc, pck = ps_next()
                      rd = [zkey(j, i), ('Zpad', j)] + ([zkey(j, i - 1)] if i > 0 else []) + [('DG31', j, q) for q in range(31)]
                      conv_diag(pc, n, pck, [DG31[:, j * 31 + q, :] for q in range(31)],
                                (lambda j, o, n: lambda q: Z[:, j, PAD + o - 30 + q:PAD + o - 30 + q + n])(j, o, n), rd)
                      y, ykk = tmp_next()
                      act(y[:, 0:n], pc[:, 0:n], AF.Identity, [pck, 'LNC'], [ykk], bias=LNC[:, CBD + j:CBD + j + 1], scale=1.0)
                      yb_, ybk = tb_next()
                      cp('dve', yb_[:, 0:n], y[:, 0:n], [ykk], [ybk])
                      cst[(j, i)] = dict(o=o, n=n, y=y, ykk=ykk, yb_=yb_, ybk=ybk)

                  def c_s2(j, i):
                      c = cst[(j, i)]; n = c['n']; y = c['y']; ykk = c['ykk']
                      pmn, pmnk = ps_next()
                      mm(pmn[:, 0:n], [(CSTb[:, GAVG, :], c['yb_'][:, 0:n])], ['CSTb', c['ybk']], pmnk)
                      tt('dve', y[:, 0:n], y[:, 0:n], pmn[:, 0:n], ALU.subtract, [ykk, pmnk], [ykk])
                      sq_, sqk = tb_next()
                      act(sq_[:, 0:n], y[:, 0:n], AF.Square, [ykk], [sqk])
                      c['sq_'] = sq_; c['sqk'] = sqk

                  def c_s3(j, i):
                      c = cst[(j, i)]; o = c['o']; n = c['n']; y = c['y']; ykk = c['ykk']
                      pv, pvk = ps_next()
                      mm(pv[:, 0:n], [(CSTb[:, GAVG, :], c['sq_'][:, 0:n])], ['CSTb', c['sqk']], pvk)
                      r, rk = tmp_next()
                      rsqrt_into(r[:, 0:n], pv[:, 0:n], 1.0, [pvk], rk, n)
                      tt('dve', y[:, 0:n], y[:, 0:n], r[:, 0:n], ALU.mult, [ykk, rk], [ykk])
                      ts('dve', y[:, 0:n], y[:, 0:n], LNC[:, CLG + j:CLG + j + 1], LNC[:, CLB + j:CLB + j + 1], ALU.mult, ALU.add, [ykk, 'LNC'], [ykk])
                      act(Y[:, 2 + j, o:o + n], y[:, 0:n], AF.Silu, [ykk], [('Y', 2 + j, i)])

                  for q in range(len(ctiles) + 2):
                      if q < len(ctiles):
                          c_s1(*ctiles[q])
                      if 1 <= q <= len(ctiles):
                          c_s2(*ctiles[q - 1])
                      if q >= 2:
                          c_s3(*ctiles[q - 2])

                  chk('B', l, pq)
                  for half in range(2):
                      wC1, wC1k = load_w(w_in[l].rearrange("(k p) n -> p k n", p=128)[:, :, 1024 + half * 256:1280 + half * 256], KD, 256)
                      for i in range(len(SUBT)):
                          o, n = cur(i)
                          for c in range(2):
                              pst, pk, _ = proj(wC1, wC1k, i, c)
                              evac_z(pst, pk, n, (2 + c) if half == 0 else c, i)
                  wC2, wC2k = load_w(w_in[l].rearrange("(k p) n -> p k n", p=128)[:, :, 1536:1792], KD, 256)
                  for i in range(len(SUBT)):
                      o, n = cur(i)
                      for j in range(2):
                          pst, pk, _ = proj(wC2, wC2k, i, j)
                          zk_ = zkey(j, i)
                          tt('dve', Z[:, j, PAD + o:PAD + o + n], Z[:, j, PAD + o:PAD + o + n], pst[:, 0:n], ALU.mult, [zk_, pk], [zk_])
                  for i in range(len(SUBT)):
                      o, n = cur(i)
                      for j in range(2):
                          pc, pck = ps_next()
                          rd = [zkey(j, i), ('Zpad', j)] + ([zkey(j, i - 1)] if i > 0 else []) + [('DG3', q, j) for q in range(3)]
                          conv_diag(pc, n, pck, [DG3[:, q * 2 + j, :] for q in range(3)],
                                    (lambda j, o, n: lambda q: Z[:, j, PAD + o - 2 + q:PAD + o - 2 + q + n])(j, o, n), rd)
                          tt('dve', Y[:, 4 + j, o:o + n], Z[:, 2 + j, PAD + o:PAD + o + n], pc[:, 0:n], ALU.mult, [zkey(2 + j, i), pck], [('Y', 4 + j, i)])
                  chk('C', l, pq)
                  wD, wDk = load_w(w_in[l].rearrange("(k p) n -> p k n", p=128)[:, :, 1792:2048], KD, 256)
                  for i in range(len(SUBT)):
                      o, n = cur(i)
                      for j in range(2):
                          pst, pk, _ = proj(wD, wDk, i, j)
                          evac_z(pst, pk, n, j, i)
                  for i in range(len(SUBT)):
                      o, n = cur(i)
                      pls = []
                      for j in range(2):
                          ntap = 4 if j == 0 else 16
                          tb0 = 6 + (0 if j == 0 else 4)
                          pc, pck = ps_next()
                          rd = [zkey(j, i), ('Zpad', j), 'CSTb'] + ([zkey(j, i - 1)] if i > 0 else [])
                          conv_diag(pc, n, pck, [CSTb[:, tb0 + q, :] for q in range(ntap)],
                                    (lambda j, o, n: lambda q: Z[:, j, PAD + o - q:PAD + o - q + n])(j, o, n), rd)
                          plb, plbk = tb_next()
                          if pq == 0 and i == 0:
                              pl, plk = tmp_next()
                              cp('act', pl[:, 0:n], pc[:, 0:n], [pck], [plk])
                              tt('dve', pl[:, HALO - o:HALO - o + 16], pl[:, HALO - o:HALO - o + 16], CNT[:, j, :], ALU.mult, [plk, 'CNT'], [plk])
                              tt('dve', plb[:, 0:n], pl[:, 0:n], Z[:, j, PAD + o:PAD + o + n], ALU.subtract, [plk, zkey(j, i)], [plbk])
                          else:
                              tt('dve', plb[:, 0:n], pc[:, 0:n], Z[:, j, PAD + o:PAD + o + n], ALU.subtract, [pck, zkey(j, i)], [plbk])
                          pw, pwk = ps_next()
                          mm(pw[:, 0:n], [(POOLBD[:, j, :], plb[:, 0:n])], ['POOLBD', plbk], pwk)
                          act(Y[:, 6 + j, o:o + n], pw[:, 0:n], AF.Identity, [pwk, 'LNC'], [('Y', 6 + j, i)], scale=LNC[:, PSC + j:PSC + j + 1], bias=0.0)

                  chk('D', l, pq)
                  st0['v'] = 126 if l == 0 else 254
                  for dc in range(KD):
                      for kb in range(4):
                          c0 = 2048 + kb * 1024 + dc * 128
                          wg, wgk = load_w(w_in[l].rearrange("(k p) n -> p k n", p=128)[:, :, c0:c0 + 128], KD, 128)
                          wb_, wbk = load_w(w_branch[l, kb].rearrange("(j p) d -> p j d", p=128)[:, :, dc * 128:(dc + 1) * 128], 2, 128)
                          for i in range(len(SUBT)):
                              o, n = cur(i)
                              acc, acck = ACC[i], ('acc', i)
                              pg, pgk, _ = proj(wg, wgk, i, 0)
                              sg, sgk = tmp_next()
                              act(sg[:, 0:n], pg[:, 0:n], AF.Sigmoid, [pgk], [sgk])
                              pb, pbk = ps_next()
                              mm(pb[:, 0:n], [(wb_[:, j, :], Y[:, 2 * kb + j, o:o + n]) for j in range(2)],
                                 [wbk, ('Y', 2 * kb, i), ('Y', 2 * kb + 1, i)], pbk)
                              if kb == 0:
                                  tt('dve', acc[:, 0:n], pb[:, 0:n], sg[:, 0:n], ALU.mult, [pbk, sgk], [acck])
                              else:
                                  tt('dve', sg[:, 0:n], pb[:, 0:n], sg[:, 0:n], ALU.mult, [pbk, sgk], [sgk])
                                  if kb < 3:
                                      tt('dve', acc[:, 0:n], acc[:, 0:n], sg[:, 0:n], ALU.add, [acck, sgk], [acck])
                                  else:
                                      tt('dve', MG[:, dc, o:o + n], acc[:, 0:n], sg[:, 0:n], ALU.add, [acck, sgk], [('MG', dc, i)])
                  chk('merge', l, pq)
                  for dc in range(KD):
                      wo, wok = load_w(w_out[l].rearrange("(k p) n -> p k n", p=128)[:, :, dc * 128:(dc + 1) * 128], KD, 128)
                      for i in range(len(SUBT)):
                          o, n = cur(i)
                          pst, pk = ps_next()
                          mm(pst[:, 0:n], [(wo[:, k, :], MG[:, k, o:o + n]) for k in range(KD)], [wok] + [('MG', k, i) for k in range(KD)], pk)
                          tt('dve', X[:, dc, o:o + n], X[:, dc, o:o + n], pst[:, 0:n], ALU.add, [xk(dc, i), pk], [xk(dc, i)])

                  chk('xmid', l, pq)
                  rmsnorm((l * 3 + 1) * KD, 'H', True, pq)
                  for (j0, nj) in JGROUPS:
                      for jj in range(nj):
                          j = j0 + jj
                          wg, wgk = load_w(w_up[l].rearrange("(k p) n -> p k n", p=128)[:, :, j * 128:(j + 1) * 128], KD, 128)
                          wv, wvk = load_w(w_up[l].rearrange("(k p) n -> p k n", p=128)[:, :, DFF + j * 128:DFF + (j + 1) * 128], KD, 128)
                          for i in range(len(SUBT)):
                              o, n = cur(i)
                              res = []
                              for (U, un, cj, ww, wwk) in ((UG, 'UG', j, wg, wgk), (UV, 'UV', NJ + j, wv, wvk)):
                                  pst, pk, _ = proj(ww, wwk, i, 0)
                                  cp('act', U[:, PAD + o:PAD + o + n], pst[:, 0:n], [pk], [(un, i)])
                                  a, ak = tmp_next()
                                  wbase = l * 132 + cj
                                  rd = [(un, i), un + 'pad', 'FWC', 'FBC'] + ([(un, i - 1)] if i > 0 else [])
                                  if un == 'UG':
                                      act(a[:, 0:n], pst[:, 0:n], AF.Identity, [pk, 'FWC', 'FBC'], [ak],
                                          scale=FWC[:, wbase + 88:wbase + 89], bias=FBC[:, l * 44 + cj:l * 44 + cj + 1])
                                  else:
                                      ts('pool', a[:, 0:n], U[:, PAD + o:PAD + o + n], FWC[:, wbase + 88:wbase + 89], FBC[:, l * 44 + cj:l * 44 + cj + 1],
                                         ALU.mult, ALU.add, rd, [ak])
                                  stt('dve', a[:, 0:n], U[:, PAD + o - 1:PAD + o - 1 + n], FWC[:, wbase + 44:wbase + 45], a[:, 0:n], ALU.mult, ALU.add, rd + [ak], [ak])
                                  stt('dve', a[:, 0:n], U[:, PAD + o - 2:PAD + o - 2 + n], FWC[:, wbase:wbase + 1], a[:, 0:n], ALU.mult, ALU.add, rd + [ak], [ak])
                                  res.append((a, ak))
                              (ga, gak), (va, vak) = res
                              sg, sgk = tmp_next()
                              act(sg[:, 0:n], ga[:, 0:n], AF.Silu, [gak], [sgk])
                              tt('dve', Y[:, jj, o:o + n], sg[:, 0:n], va[:, 0:n], ALU.mult, [sgk, vak], [('Y', jj, i)])
                      for dc in range(KD):
                          wd, wdk = load_w(w_down[l].rearrange("(k p) n -> p k n", p=128)[:, j0:j0 + nj, dc * 128:(dc + 1) * 128], nj, 128)
                          for i in range(len(SUBT)):
                              o, n = cur(i)
                              pst, pk = ps_next()
                              mm(pst[:, 0:n], [(wd[:, jj, :], Y[:, jj, o:o + n]) for jj in range(nj)], [wdk] + [('Y', jj, i) for jj in range(nj)], pk)
                              tt('dve', X[:, dc, o:o + n], X[:, dc, o:o + n], pst[:, 0:n], ALU.add, [xk(dc, i), pk], [xk(dc, i)])

                  chk('ffn', l, pq)
                  rmsnorm((l * 3 + 2) * KD, 'H', False)
                  ptkeys = [zkey(jz, iz) for jz in range(2) for iz in range(len(SUBT))]
                  cdma(Z[:, 0:2, PAD:PAD + T], pT[pq, l].rearrange("(j p) t -> p j t", p=128), ptkeys, 'PT')
                  for dc in range(KD):
                      wg, wgk = load_w(w_pg[l].rearrange("(k p) n -> p k n", p=128)[:, :, dc * 128:(dc + 1) * 128], KD, 128)
                      wp, wpk = load_w(w_pp[l].rearrange("(j p) n -> p j n", p=128)[:, :, dc * 128:(dc + 1) * 128], 2, 128)
                      for i in range(len(SUBT)):
                          o, n = cur(i)
                          pg, pgk, _ = proj(wg, wgk, i, 0)
                          sg, sgk = tmp_next()
                          act(sg[:, 0:n], pg[:, 0:n], AF.Sigmoid, [pgk], [sgk])
                          pp, ppk = ps_next()
                          mm(pp[:, 0:n], [(wp[:, j, :], Z[:, j, PAD + o:PAD + o + n]) for j in range(2)], [wpk, zkey(0, i), zkey(1, i)], ppk)
                          tt('dve', sg[:, 0:n], pp[:, 0:n], sg[:, 0:n], ALU.mult, [ppk, sgk], [sgk])
                          tt('dve', X[:, dc, o:o + n], X[:, dc, o:o + n], sg[:, 0:n], ALU.add, [xk(dc, i), sgk], [xk(dc, i)])

              chk('end', L - 1, pq)
              st0['v'] = 0
              fin_sub = [(256, 256, 0), (512, 512, 1), (1024, 256, 2)]
              for (o, n, i) in fin_sub:
                  pst, pk = ps_next()
                  for k in range(KD):
                      tb, tk = tb_next()
                      act(tb[:, 0:n], X[:, k, o:o + n], AF.Square, [xk(k, i)], [tk])
                      S.add('pe', (lambda tb, k, pst, o, n: lambda e: e.matmul(pst[:, 0:n], lhsT=CSTb[:, ONES, :], rhs=tb[:, 0:n], start=(k == 0), stop=(k == KD - 1)))(tb, k, pst, o, n),
                            reads=[tk, 'CSTb'], writes=[pk])
                  r, rk = tmp_next()
                  rsqrt_into(r[:, 0:n], pst[:, 0:n], 1.0 / D, [pk], rk, n)
                  for k in range(KD):
                      ot, otk = tmp_next()
                      stt('dve', ot[:, 0:n], X[:, k, o:o + n], GC[:, L * 3 * KD + k:L * 3 * KD + k + 1], r[:, 0:n], ALU.mult, ALU.mult, [xk(k, i), rk, 'GC'], [otk])
                      okey = ('out', pq, k, o)
                      S.add('sp', (lambda ot, k, o, n, pq: lambda e: e.dma_start(out=outT[pq, k * 128:(k + 1) * 128, o - 256:o - 256 + n], in_=ot[:, 0:n]))(ot, k, o, n, pq),
                            reads=[otk], writes=[okey], dma_key=('o', k))
                      outs.append(S.ops[-1])

        except _Stop:
            pass
        if debug is not None:
            allkeys = list(S.last_writer.keys())
            for nm, (dap, tns) in dbg.items():
                S.add('sp', (lambda dap, tns: lambda e: e.dma_start(out=dap[:], in_=tns[:]))(dap, tns), reads=allkeys, writes=[('dbg', nm)], dma_key=('dbg', nm))
                outs.append(S.ops[-1])
        S.finalize(nc, st)
        with nc.Block() as block0:
            @block0.gpsimd
            def _(e):
                for sem in S.sems.values():
                    e.sem_clear(sem)
                for sem in S.sems.values():
                    e.wait_op(sem, 0, "sem-eq")
        with nc.Block() as block:
            @block.sync
            def _(e):
                fw = {}
                for op in outs:
                    fw[op.token[0]] = max(fw.get(op.token[0], 0), op.token[1])
                S.emit_engine('sp', e, list(fw.items()))

            @block.scalar
            def _(e): S.emit_engine('act', e)

            @block.vector
            def _(e): S.emit_engine('dve', e)

            @block.gpsimd
            def _(e): S.emit_engine('pool', e)

            @block.tensor
            def _(e): S.emit_engine('pe', e)
    return nc


def _cols(v):
    return np.ascontiguousarray(np.asarray(v, np.float32).reshape(-1, 128).T)


def prepare(x, p, g_mix, w_in, gmlp_ln_g, gmlp_ln_b, gmlp_w_s, gmlp_b_s,
           conf_w_dw, conf_b_dw, conf_ln_g, conf_ln_b, short_w, pool_w, pool_scale,
           w_branch, w_out, g_ffn, ffn_w_up, ffn_w_conv, ffn_b_conv, ffn_w_down,
           g_ple, ple_w_gate, ple_w_proj, g_final):
    f = lambda a: np.ascontiguousarray(np.asarray(a, np.float32))
    x = f(x); p = f(p)
    B, SEQ, _ = x.shape
    gcols = np.concatenate([_cols(v[l]) for l in range(L) for v in (g_mix, g_ffn, g_ple)] + [_cols(g_final)], axis=1)
    lncols = np.concatenate([_cols(np.asarray(v)[l].reshape(-1)) for l in range(L)
                             for v in (gmlp_ln_g, gmlp_ln_b, conf_b_dw, conf_ln_g, conf_ln_b, pool_scale)], axis=1)
    shortw = np.concatenate([_cols(np.asarray(short_w)[l, q]) for l in range(L) for q in range(3)], axis=1)
    confw = np.concatenate([_cols(np.asarray(conf_w_dw)[l, q]) for l in range(L) for q in range(31)], axis=1)
    ffnwc = np.concatenate([_cols(np.asarray(ffn_w_conv)[l, q]) for l in range(L) for q in range(3)], axis=1)
    ffnbc = np.concatenate([_cols(np.asarray(ffn_b_conv)[l]) for l in range(L)], axis=1)
    gwT = f(np.transpose(np.asarray(gmlp_w_s), (0, 1, 3, 2)))
    gbs = f(np.asarray(gmlp_b_s).reshape(1, -1))
    pw = np.asarray(pool_w, np.float32)
    poolbd = np.zeros((L, 2, 128, 128), np.float32)
    for l in range(L):
        for j in range(2):
            poolbd[l, j, :64, :64] = pw[l, 2 * j]
            poolbd[l, j, 64:, 64:] = pw[l, 2 * j + 1]
    consts = np.zeros((26, 128, 128), np.float32)
    consts[0] = np.eye(128)
    consts[1] = np.triu(np.ones((128, 128)))
    consts[2][:, :64] = 1.0
    consts[3][:, 64:] = 1.0
    consts[4][:64, :64] = 1.0 / 64; consts[4][64:, 64:] = 1.0 / 64
    consts[5] = 1.0
    pidx = np.arange(128)
    for j in range(2):
        ntap = 4 if j == 0 else 16
        tb0 = 6 + (0 if j == 0 else 4)
        win = np.where(pidx < 64, WINS[2 * j], WINS[2 * j + 1]).astype(np.float32)
        for q in range(ntap):
            consts[tb0 + q][pidx, pidx] = np.where(q < win, 1.0 / win, 0.0)
    in_maps = []
    for c in range(8):
        b, seg = c // 4, c % 4
        xT = np.zeros((NPASS, D, T), np.float32)
        pT = np.zeros((NPASS, L, 256, T), np.float32)
        mask = np.full((128, HALO), 0.0 if seg == 0 else 1.0, np.float32)
        cnt = np.ones((128, 2, 16), np.float32)
        tpos16 = seg * 2048 + np.arange(16)
        for j in range(2):
            win = np.where(pidx < 64, WINS[2 * j], WINS[2 * j + 1]).astype(np.float32)[:, None]
            cnt[:, j, :] = win / np.minimum(tpos16[None, :] + 1, win)
        for q in range(NPASS):
            s = seg * 2048 + q * TOUT - HALO
            lo = max(s, 0)
            xT[q, :, lo - s:] = x[b, lo:s + T].T
            for l in range(L):
                pT[q, l, :, lo - s:] = p[l, b, lo:s + T].T
        in_maps.append({
            "xT": xT, "pT": pT, "mask": mask, "cnt": cnt,
            "w_in": f(w_in), "w_branch": f(w_branch), "w_out": f(w_out), "w_up": f(ffn_w_up), "w_down": f(ffn_w_down),
            "w_pg": f(ple_w_gate), "w_pp": f(ple_w_proj), "gcols": f(gcols), "lncols": f(lncols), "shortw": f(shortw),
            "confw": f(confw), "ffnwc": f(ffnwc), "ffnbc": f(ffnbc), "gwT": gwT, "gbs": gbs, "poolbd": poolbd, "consts": consts,
        })
    return in_maps


def kernel(**inputs):
    in_maps = prepare(**inputs)
    B, SEQ = 2, 8192
    nc = build_nc()
    res = run_bass_kernel_spmd(nc, in_maps, core_ids=list(range(8)))
    out = np.zeros((B, SEQ, D), np.float32)
    for c in range(8):
        b, seg = c // 4, c % 4
        o = res.results[c]["outT"]
        for q in range(NPASS):
            s = seg * 2048 + q * TOUT
            out[b, s:s + TOUT, :] = o[q].T
    return out
```

```python
from contextlib import ExitStack
import numpy as np
import concourse.bass as bass
import concourse.mybir as mybir
from concourse.bass_utils import run_bass_kernel_spmd

F32 = mybir.dt.float32
BF16 = mybir.dt.bfloat16
AF = mybir.ActivationFunctionType
ALU = mybir.AluOpType
AX = mybir.AxisListType

L = 2
D = 1024
KD = 8
HALO = 256
TOUT = 1024
T = HALO + TOUT
PAD = 32
NPASS = 2
SUBT = [(0, 512), (512, 512), (1024, 256)]
DFF = 2816
NJ = 22
JGROUPS = [(0, 8), (8, 8), (16, 6)]
EPS = 1e-6
WINS = (2, 4, 8, 16)


class _Op:
    __slots__ = ('eng', 'emit', 'is_dma', 'dma_key', 'deps', 'signal', 'token', 'waits', 'idx', 'snapshot', 'stage', 'hoist')


class Sched:
    def __init__(self):
        self.ops = []
        self.last_writer = {}
        self.readers = {}
        self.stage = ''

    def add(self, eng, emit, reads=(), writes=(), dma_key=None, hoist=False):
        op = _Op()
        op.hoist = hoist
        op.eng = eng
        op.emit = emit
        op.is_dma = dma_key is not None
        op.dma_key = dma_key
        op.signal = op.is_dma
        op.token = None
        op.idx = len(self.ops)
        op.stage = self.stage
        deps = {}
        for k in reads:
            w = self.last_writer.get(k)
            if w is not None:
                deps[w.idx] = (w, True)
        for k in writes:
            w = self.last_writer.get(k)
            if w is not None:
                deps[w.idx] = (w, True)
            for r in self.readers.get(k, ()):
                if r.idx not in deps:
                    deps[r.idx] = (r, False)
        for k in reads:
            self.readers.setdefault(k, []).append(op)
        for k in writes:
            self.last_writer[k] = op
            self.readers[k] = []
        op.deps = []
        for i in sorted(deps):
            d, hard = deps[i]
            if not d.is_dma and not op.is_dma and d.eng == op.eng:
                if op.eng == 'pe' or not hard:
                    continue
            d.signal = True
            op.deps.append(d)
        self.ops.append(op)
        return op

    def finalize(self, nc, stack):
        keyed = []
        for op in self.ops:
            if op.hoist:
                m = max([d.idx for d in op.deps], default=-1)
                keyed.append(((m, 1, op.idx), op))
            else:
                keyed.append(((op.idx, 0, 0), op))
        keyed.sort(key=lambda kv: kv[0])
        self.ops = [op for _, op in keyed]
        self.sems = {}
        cnt = {}
        for op in self.ops:
            if not op.signal:
                continue
            key = ('dma', op.dma_key) if op.is_dma else ('eng', op.eng)
            if key not in self.sems:
                self.sems[key] = stack.enter_context(nc.semaphore("s%d" % len(self.sems)))
                cnt[key] = 0
            cnt[key] += 16 if op.is_dma else 1
            op.token = (key, cnt[key])
        known = {}
        for op in self.ops:
            K = known.setdefault(op.eng, {})
            waits = {}
            for d in op.deps:
                key, val = d.token
                if K.get(key, 0) >= val:
                    continue
                waits[key] = max(waits.get(key, 0), val)
                for s, v in d.snapshot.items():
                    if K.get(s, 0) < v:
                        K[s] = v
                K[key] = val
            op.waits = list(waits.items())
            op.snapshot = dict(K)

    def emit_engine(self, name, eng, final_waits=()):
        for op in self.ops:
            if op.eng != name:
                continue
            for key, val in op.waits:
                eng.wait_ge(self.sems[key], val)
            ins = op.emit(eng)
            if op.signal:
                ins.then_inc(self.sems[op.token[0]], 16 if op.is_dma else 1)
        for key, val in final_waits:
            eng.wait_ge(self.sems[key], val)


class _Stop(Exception):
    pass


def build_nc(debug=None):
    nc = bass.Bass("TRN2", target_bir_lowering=False)

    def din(name, shape):
        return nc.dram_tensor(name, list(shape), F32, kind="ExternalInput").ap()

    xT = din("xT", [NPASS, D, T])
    pT = din("pT", [NPASS, L, 256, T])
    maskd = din("mask", [128, HALO])
    cntd = din("cnt", [128, 2, 16])
    w_in = din("w_in", [L, D, 6144])
    w_branch = din("w_branch", [L, 4, 256, D])
    w_out = din("w_out", [L, D, D])
    w_up = din("w_up", [L, D, 2 * DFF])
    w_down = din("w_down", [L, DFF, D])
    w_pg = din("w_pg", [L, D, D])
    w_pp = din("w_pp", [L, 256, D])
    gcols_d = din("gcols", [128, L * 3 * KD + KD])
    lncols_d = din("lncols", [128, L * 6 * 2])
    shortw_d = din("shortw", [128, L * 3 * 2])
    confw_d = din("confw", [128, L * 31 * 2])
    ffnwc_d = din("ffnwc", [128, L * 3 * 44])
    ffnbc_d = din("ffnbc", [128, L * 44])
    gwT_d = din("gwT", [L, 4, 128, 128])
    gbs_d = din("gbs", [1, L * 4 * 128])
    poolbd_d = din("poolbd", [L, 2, 128, 128])
    consts_d = din("consts", [6 + 20, 128, 128])
    outT = nc.dram_tensor("outT", [NPASS, D, TOUT], F32, kind="ExternalOutput").ap()

    S = Sched()
    st = ExitStack()
    with st:
        def sb(name, shape, dt=F32):
            return st.enter_context(nc.sbuf_tensor(name, list(shape), dt))

        X = sb("X", [128, KD, T])
        H = sb("H", [128, KD, T], BF16)
        Z = sb("Z", [128, 4, PAD + T], BF16)
        Y = sb("Y", [128, 8, T], BF16)
        MG = sb("MG", [128, KD, T], BF16)
        UG = sb("UG", [128, PAD + T])
        UV = sb("UV", [128, PAD + T])
        MASK = sb("MASK", [128, HALO])
        CNT = sb("CNT", [128, 2, 16])
        ZN2 = sb("ZN2", [128, 12, 128], BF16)
        BIAS2 = sb("BIAS2", [128, 2, 128])
        GWT = sb("GWT", [128, 4, 128], BF16)
        GWTs = sb("GWTs", [128, 4, 128], BF16)
        GBSb = sb("GBSb", [1, L * 4 * 128], BF16)
        POOLBD = sb("POOLBD", [128, 2, 128], BF16)
        CST = sb("CST", [128, 2, 128])
        CSTb = sb("CSTb", [128, 26, 128], BF16)
        DG31 = sb("DG31", [128, 62, 128], BF16)
        DG3 = sb("DG3", [128, 6, 128], BF16)
        GC = sb("GC", [128, L * 3 * KD + KD])
        LNC = sb("LNC", [128, L * 12])
        SHW = sb("SHW", [128, L * 6])
        CFW = sb("CFW", [128, L * 62])
        FWC = sb("FWC", [128, L * 132])
        FBC = sb("FBC", [128, L * 44])
        EPSC = sb("EPSC", [128, 1])
        NT = 8
        TMP = [sb("TMP%d" % i, [128, 512]) for i in range(NT)]
        TB = [sb("TB%d" % i, [128, 512], BF16) for i in range(4)]
        COL = sb("COL", [128, 8])
        COLS = sb("COLS", [128, 16])
        COLQ = sb("COLQ", [128, 16])
        COLM = sb("COLM", [128, 16])
        NWB = 5
        WB = [sb("WB%d" % i, [128, 2048], BF16) for i in range(NWB)]
        ACC = [sb("ACC%d" % i, [128, 512]) for i in range(len(SUBT))]
        PS = [st.enter_context(nc.psum_tensor("PS%d" % i, [128, 512], F32)) for i in range(8)]

        IDENT, TRI, SELA, SELB, GAVG, ONES = 0, 1, 2, 3, 4, 5
        state = {'ps': 0, 'ws': 0, 'wb': 0, 'tmp': 0, 'tb': 0, 'dmaq': 0}

        def ps_next():
            i = state['ps']; state['ps'] = (i + 1) % 8
            return PS[i], ('ps', i)

        def tmp_next():
            i = state['tmp']; state['tmp'] = (i + 1) % NT
            return TMP[i], ('tmp', i)

        def tb_next():
            i = state['tb']; state['tb'] = (i + 1) % 4
            return TB[i], ('tb', i)

        def dma(out_ap, in_ap, wkey, reads=()):
            S.add('sp', lambda e: e.dma_start(out=out_ap, in_=in_ap), reads=list(reads), writes=[wkey], dma_key=wkey)

        def load_w(src3, kc, ncol):
            j = state['wb']; state['wb'] = (j + 1) % NWB
            n = kc * ncol
            bview = WB[j][:, 0:n].rearrange("p (k n) -> p k n", k=kc)
            cdma(bview, src3, [('wb', j)], ('wb', j), hoist=True)
            return bview, ('wb', j)

        def cdma(out_ap, in_ap, wkeys, dkey, hoist=False):
            S.add('pool', lambda e: e.dma_start(out=out_ap, in_=in_ap), writes=list(wkeys), dma_key=dkey, hoist=hoist)

        def mm(out_ap, pairs, reads, wkey):
            def f(e):
                n = len(pairs)
                for q, (l, r) in enumerate(pairs):
                    ins = e.matmul(out_ap, lhsT=l, rhs=r, start=(q == 0), stop=(q == n - 1))
                return ins
            S.add('pe', f, reads=list(reads), writes=[wkey])

        def act(out, in_, func, reads, writes, **kw):
            S.add('act', lambda e: e.activation(out=out, in_=in_, func=func, **kw), reads=list(reads), writes=list(writes))

        def tt(eng, out, a, b, op, reads, writes):
            S.add(eng, lambda e: e.tensor_tensor(out=out, in0=a, in1=b, op=op), reads=list(reads), writes=list(writes))

        def ts(eng, out, a, s1, s2, op0, op1, reads, writes):
            if s2 is None:
                S.add(eng, lambda e: e.tensor_scalar(out=out, in0=a, scalar1=s1, scalar2=None, op0=op0), reads=list(reads), writes=list(writes))
            else:
                S.add(eng, lambda e: e.tensor_scalar(out=out, in0=a, scalar1=s1, scalar2=s2, op0=op0, op1=op1), reads=list(reads), writes=list(writes))

        def stt(eng, out, a, sc, b, op0, op1, reads, writes):
            S.add(eng, lambda e: e.scalar_tensor_tensor(out=out, in0=a, scalar=sc, in1=b, op0=op0, op1=op1), reads=list(reads), writes=list(writes))

        def cp(eng, out, in_, reads, writes):
            if eng == 'act':
                S.add(eng, lambda e: e.activation(out=out, in_=in_, func=AF.Copy), reads=list(reads), writes=list(writes))
            else:
                S.add(eng, lambda e: e.tensor_copy(out=out, in_=in_), reads=list(reads), writes=list(writes))

        def rsqrt_into(out, in_ap, scale, reads, wkey, n):
            ts('dve', out, in_ap, scale, EPS, ALU.mult, ALU.add, reads, [wkey])
            act(out, out, AF.Sqrt, [wkey], [wkey])
            S.add('dve', lambda e: e.reciprocal(out=out, in_=out), reads=[wkey], writes=[wkey])

        dma(CST[:], consts_d[0:2].rearrange("c p n -> p c n"), 'CST')
        cdma(CSTb[:], consts_d.rearrange("c p n -> p c n"), ['CSTb'], 'CSTb')
        dma(GC[:], gcols_d, 'GC'); dma(LNC[:], lncols_d, 'LNC'); dma(SHW[:], shortw_d, 'SHW')
        dma(CFW[:], confw_d, 'CFW'); dma(FWC[:], ffnwc_d, 'FWC'); dma(FBC[:], ffnbc_d, 'FBC')
        cdma(GBSb[:], gbs_d, ['GBSb'], 'GBSb')
        dma(MASK[:], maskd, 'MASK')
        dma(CNT[:], cntd, 'CNT')
        S.add('pool', lambda e: e.memset(EPSC[:], EPS), writes=['EPSC'])
        S.add('pool', lambda e: e.memset(ZN2[:], 0.0), writes=['ZN2'])
        for zi in range(4):
            S.add('pool', (lambda zi: lambda e: e.memset(Z[:, zi, 0:PAD], 0.0))(zi), writes=[('Zpad', zi)])
        S.add('pool', lambda e: e.memset(UG[:, 0:PAD], 0.0), writes=['UGpad'])
        S.add('pool', lambda e: e.memset(UV[:, 0:PAD], 0.0), writes=['UVpad'])

        def xk(k, i): return ('X', k, i)

        st0 = {'v': 0}

        def cur(i):
            o, n = SUBT[i]
            if i == 0:
                return st0['v'], n - st0['v']
            return o, n

        def rmsnorm(gbase, hk, masked, pq=0):
            for i in range(len(SUBT)):
                o, n = cur(i)
                pst, pk = ps_next()
                tbs = []
                for k in range(KD):
                    tb, tk = tb_next()
                    act(tb[:, 0:n], X[:, k, o:o + n], AF.Square, [xk(k, i)], [tk])
                    tbs.append((tb, tk))
                    if len(tbs) == 4 or k == KD - 1:
                        k0 = k - len(tbs) + 1
                        for q, (tb2, tk2) in enumerate(tbs):
                            kk = k0 + q
                            S.add('pe', (lambda tb2, kk, pst, n: lambda e: e.matmul(pst[:, 0:n], lhsT=CSTb[:, ONES, :], rhs=tb2[:, 0:n], start=(kk == 0), stop=(kk == KD - 1)))(tb2, kk, pst, n),
                                  reads=[tk2, 'CSTb'], writes=[pk])
                        tbs = []
                r, rk = tmp_next()
                rsqrt_into(r[:, 0:n], pst[:, 0:n], 1.0 / D, [pk], rk, n)
                if masked and pq == 0 and i == 0:
                    tt('dve', r[:, 0:HALO - o], r[:, 0:HALO - o], MASK[:, o:HALO], ALU.mult, [rk, 'MASK'], [rk])
                for k in range(KD):
                    stt('dve', H[:, k, o:o + n], X[:, k, o:o + n], GC[:, gbase + k:gbase + k + 1], r[:, 0:n], ALU.mult, ALU.mult,
                        [xk(k, i), rk, 'GC'], [(hk, k, i)])

        def hreads(i): return [('H', k, i) for k in range(KD)]

        def proj(wcols, wk, i, c):
            o, n = cur(i)
            pst, pk = ps_next()
            mm(pst[:, 0:n], [(wcols[:, k, c * 128:(c + 1) * 128], H[:, k, o:o + n]) for k in range(KD)], hreads(i) + [wk], pk)
            return pst, pk, n

        def conv_diag(pst, n, pk, mats, src_fn, reads):
            mm(pst[:, 0:n], [(mats[q], src_fn(q)) for q in range(len(mats))], reads, pk)

        outs = []

        def chk(tag, l, pq):
            S.stage = 'after_%s_L%d_P%d' % (tag, l, pq)
            if debug is not None and debug == (tag, l) and pq == 0:
                raise _Stop()

        dbg = {}
        if debug is not None:
            for nm, tns, dt in (("dX", X, F32), ("dH", H, BF16), ("dY", Y, BF16), ("dMG", MG, BF16), ("dZ", Z, BF16), ("dUG", UG, F32)):
                dbg[nm] = (nc.dram_tensor(nm, list(tns.shape), dt, kind="ExternalOutput").ap(), tns)
        try:
          for pq in range(NPASS):
              for k in range(KD):
                  for i, (o, n) in enumerate(SUBT):
                      dma(X[:, k, o:o + n], xT[pq, k * 128:(k + 1) * 128, o:o + n], xk(k, i))
              for l in range(L):
                  lc = l * 12
                  LNG, LNB, CBD, CLG, CLB, PSC = [lc + 2 * q for q in range(6)]
                  st0['v'] = 0 if l == 0 else 128
                  rmsnorm((l * 3 + 0) * KD, 'H', True, pq)
                  for j in range(2):
                      for q in range(31):
                          ts('dve', DG31[:, j * 31 + q, :], CST[:, IDENT, :], CFW[:, l * 62 + q * 2 + j:l * 62 + q * 2 + j + 1], None, ALU.mult, None,
                             ['CST', 'CFW'], [('DG31', j, q)])

                  cdma(GWTs[:], gwT_d[l].rearrange("g s t -> s g t"), ['GWTs'], 'GWTs', hoist=True)
                  for g in range(4):
                      tt('dve', GWT[:, g, :], GWTs[:, g, :], CSTb[:, TRI, :], ALU.mult, ['GWTs', 'CSTb'], [('GWT', g)])
                  cdma(POOLBD[:], poolbd_d[l].rearrange("j c d -> c j d"), ['POOLBD'], 'POOLBD', hoist=True)
                  for q in range(3):
                      for j in range(2):
                          ts('dve', DG3[:, q * 2 + j, :], CST[:, IDENT, :], SHW[:, l * 6 + q * 2 + j:l * 6 + q * 2 + j + 1], None, ALU.mult, None,
                             ['CST', 'SHW'], [('DG3', q, j)])
                  for j in range(2):
                      p1, k1 = ps_next()
                      mm(p1[:, 0:128], [(CSTb[:, SELA, :], GWT[:, 2 * j, :]), (CSTb[:, SELB, :], GWT[:, 2 * j + 1, :])],
                         ['CSTb', ('GWT', 2 * j), ('GWT', 2 * j + 1)], k1)
                      p2, k2 = ps_next()
                      b0 = (l * 4 + 2 * j) * 128
                      mm(p2[:, 0:128], [(CSTb[0:1, SELA, :], GBSb[0:1, b0:b0 + 128]), (CSTb[0:1, SELB, :], GBSb[0:1, b0 + 128:b0 + 256])],
                         ['CSTb', 'GBSb'], k2)
                      t2, tk2 = tmp_next()
                      cp('act', t2[:, 0:128], p2[:, 0:128], [k2], [tk2])
                      stt('dve', BIAS2[:, j, :], p1[:, 0:128], LNC[:, LNB + j:LNB + j + 1], t2[:, 0:128], ALU.mult, ALU.add,
                          [k1, tk2, 'LNC'], [('BIAS2', j)])


                  def zkey(zi, i): return ('Z', zi, i)

                  def evac_z(pst, pk, n, zi, i):
                      o = cur(i)[0]
                      cp('act', Z[:, zi, PAD + o:PAD + o + n], pst[:, 0:n], [pk], [zkey(zi, i)])

                  def gelu_from_psum(pst, pk, n, out_ap, wkeys):
                      act(out_ap, pst[:, 0:n], AF.Gelu_apprx_tanh, [pk], wkeys)

                  chk('h', l, pq)
                  wA, wAk = load_w(w_in[l].rearrange("(k p) n -> p k n", p=128)[:, :, 0:256], KD, 256)
                  wV, wVk = load_w(w_in[l].rearrange("(k p) n -> p k n", p=128)[:, :, 256:512], KD, 256)
                  st0['v'] = 126 if l == 0 else 254
                  for i in range(len(SUBT)):
                      o, n = cur(i)
                      for j in range(2):
                          pst, pk, _ = proj(wA, wAk, i, j)
                          gelu_from_psum(pst, pk, n, Y[:, j, o:o + n], [('Y', j, i)])
                  st0['v'] = 0 if l == 0 else 128
                  def gzb(ch):
                      return Z[:, ch // 5, PAD + (ch % 5) * 256:PAD + (ch % 5) * 256 + 256]

                  def gzkeys(ch):
                      return [zkey(ch // 5, iz) for iz in range(len(SUBT))]

                  def v_front(ch):
                      t0 = ch * 128
                      i = t0 // 512
                      pst, pk = ps_next()
                      mm(pst[:, 0:256], [(H[:, k, t0:t0 + 128], wV[:, k, 0:256]) for k in range(KD)], hreads(i) + [wVk], pk)
                      gz, gk = tmp_next()
                      gelu_from_psum(pst, pk, 256, gz[:, 0:256], [gk])
                      sq, sk = tmp_next()
                      act(sq[:, 0:256], gz[:, 0:256], AF.Square, [gk], [sk])
                      S.add('dve', (lambda gz, ch: lambda e: e.reduce_sum(out=COLS[:, ch:ch + 1], in_=gz[:, 0:256], axis=AX.X))(gz, ch), reads=[gk], writes=[('COLS', ch)])
                      S.add('dve', (lambda sq, ch: lambda e: e.reduce_sum(out=COLQ[:, ch:ch + 1], in_=sq[:, 0:256], axis=AX.X))(sq, ch), reads=[sk], writes=[('COLQ', ch)])
                      cp('pool', gzb(ch), gz[:, 0:256], [gk], gzkeys(ch) + [('GZB', ch)])

                  def v_stats(chs):
                      c0, c1 = chs[0], chs[-1] + 1
                      rs = [('COLS', c) for c in chs]; rq = [('COLQ', c) for c in chs]
                      ts('dve', COLS[:, c0:c1], COLS[:, c0:c1], 1.0 / 256, None, ALU.mult, None, rs, ['MEAN'])
                      tt('dve', COLM[:, c0:c1], COLS[:, c0:c1], COLS[:, c0:c1], ALU.mult, ['MEAN'], ['M2'])
                      stt('dve', COLQ[:, c0:c1], COLQ[:, c0:c1], 1.0 / 256, COLM[:, c0:c1], ALU.mult, ALU.subtract, rq + ['M2'], ['RSTD'])
                      ts('dve', COLQ[:, c0:c1], COLQ[:, c0:c1], 1.0, EPS, ALU.mult, ALU.add, ['RSTD'], ['RSTD'])
                      act(COLQ[:, c0:c1], COLQ[:, c0:c1], AF.Sqrt, ['RSTD'], ['RSTD'])
                      S.add('dve', lambda e: e.reciprocal(out=COLQ[:, c0:c1], in_=COLQ[:, c0:c1]), reads=['RSTD'], writes=['RSTD'])

                  def v_back(ch):
                      t0 = ch * 128
                      i = t0 // 512
                      zb = ch % 3
                      for g in range(4):
                          ts('dve', ZN2[:, zb * 4 + g, (g % 2) * 64:(g % 2) * 64 + 64], gzb(ch)[:, g * 64:(g + 1) * 64], COLS[:, ch:ch + 1], COLQ[:, ch:ch + 1],
                             ALU.subtract, ALU.mult, gzkeys(ch) + [('GZB', ch), 'MEAN', 'RSTD', 'ZN2'], [('ZN2', zb, g)])
                      for j in range(2):
                          pm, pmk = ps_next()
                          mm(pm[:, 0:128], [(ZN2[:, zb * 4 + 2 * j, :], GWT[:, 2 * j, :]), (ZN2[:, zb * 4 + 2 * j + 1, :], GWT[:, 2 * j + 1, :])],
                             [('ZN2', zb, 2 * j), ('ZN2', zb, 2 * j + 1), ('GWT', 2 * j), ('GWT', 2 * j + 1)], pmk)
                          m, mk = tmp_next()
                          stt('dve', m[:, 0:128], pm[:, 0:128], LNC[:, LNG + j:LNG + j + 1], BIAS2[:, j, :], ALU.mult, ALU.add,
                              [pmk, 'LNC', ('BIAS2', j)], [mk])
                          yk = ('Y', j, i)
                          tt('dve', Y[:, j, t0:t0 + 128], Y[:, j, t0:t0 + 128], m[:, 0:128], ALU.mult, [yk, mk], [yk])

                  chs = list(range(st0['v'] // 128, T // 128))
                  for ch in chs:
                      v_front(ch)
                  v_stats(chs)
                  for ch in chs:
                      v_back(ch)

                  chk('A', l, pq)
                  st0['v'] = 96 if l == 0 else 224
                  wB, wBk = load_w(w_in[l].rearrange("(k p) n -> p k n", p=128)[:, :, 512:768], KD, 256)
                  wBg, wBgk = load_w(w_in[l].rearrange("(k p) n -> p k n", p=128)[:, :, 768:1024], KD, 256)
                  for i in range(len(SUBT)):
                      o, n = cur(i)
                      for j in range(2):
                          pg, pgk, _ = proj(wBg, wBgk, i, j)
                          sg, sgk = tmp_next()
                          act(sg[:, 0:n], pg[:, 0:n], AF.Sigmoid, [pgk], [sgk])
                          pa, pak, _ = proj(wB, wBk, i, j)
                          tt('dve', Z[:, j, PAD + o:PAD + o + n], pa[:, 0:n], sg[:, 0:n], ALU.mult, [pak, sgk], [zkey(j, i)])
                  ctiles = [(j, i) for j in range(2) for i in range(len(SUBT))]
                  cst = {}

                  def c_s1(j, i):
                      o, n = cur(i)
                      pc, pck = ps_next()
                      rd = [zkey(j, i), ('Zpad', j)] + ([zkey(j, i - 1)] if i > 0 else []) + [('DG31', j, q) for q in range(31)]
                      conv_diag(pc, n, pck, [DG31[:, j * 31 + q, :] for q in range(31)],
                                (lambda j, o, n: lambda q: Z[:, j, PAD + o - 30 + q:PAD + o - 30 + q + n])(j, o, n), rd)
                      y, ykk = tmp_next()
                      act(y[:, 0:n], pc[:, 0:n], AF.Identity, [pck, 'LNC'], [ykk], bias=LNC[:, CBD + j:CBD + j + 1], scale=1.0)
                      yb_, ybk = tb_next()
                      cp('dve', yb_[:, 0:n], y[:, 0:n], [ykk], [ybk])
                      cst[(j, i)] = dict(o=o, n=n, y=y, ykk=ykk, yb_=yb_, ybk=ybk)

                  def c_s2(j, i):
                      c = cst[(j, i)]; n = c['n']; y = c['y']; ykk = c['ykk']
                      pmn, pmnk = ps_next()
                      mm(pmn[:, 0:n], [(CSTb[:, GAVG, :], c['yb_'][:, 0:n])], ['CSTb', c['ybk']], pmnk)
                      tt('dve', y[:, 0:n], y[:, 0:n], pmn[:, 0:n], ALU.subtract, [ykk, pmnk], [ykk])
                      sq_, sqk = tb_next()
                      act(sq_[:, 0:n], y[:, 0:n], AF.Square, [ykk], [sqk])
                      c['sq_'] = sq_; c['sqk'] = sqk

                  def c_s3(j, i):
                      c = cst[(j, i)]; o = c['o']; n = c['n']; y = c['y']; ykk = c['ykk']
                      pv, pvk = ps_next()
                      mm(pv[:, 0:n], [(CSTb[:, GAVG, :], c['sq_'][:, 0:n])], ['CSTb', c['sqk']], pvk)
                      r, rk = tmp_next()
                      rsqrt_into(r[:, 0:n], pv[:, 0:n], 1.0, [pvk], rk, n)
                      tt('dve', y[:, 0:n], y[:, 0:n], r[:, 0:n], ALU.mult, [ykk, rk], [ykk])
                      ts('dve', y[:, 0:n], y[:, 0:n], LNC[:, CLG + j:CLG + j + 1], LNC[:, CLB + j:CLB + j + 1], ALU.mult, ALU.add, [ykk, 'LNC'], [ykk])
                      act(Y[:, 2 + j, o:o + n], y[:, 0:n], AF.Silu, [ykk], [('Y', 2 + j, i)])

                  for q in range(len(ctiles) + 2):
                      if q < len(ctiles):
                          c_s1(*ctiles[q])
                      if 1 <= q <= len(ctiles):
                          c_s2(*ctiles[q - 1])
                      if q >= 2:
                          c_s3(*ctiles[q - 2])

                  chk('B', l, pq)
                  for half in range(2):
                      wC1, wC1k = load_w(w_in[l].rearrange("(k p) n -> p k n", p=128)[:, :, 1024 + half * 256:1280 + half * 256], KD, 256)
                      for i in range(len(SUBT)):
                          o, n = cur(i)
                          for c in range(2):
                              pst, pk, _ = proj(wC1, wC1k, i, c)
                              evac_z(pst, pk, n, (2 + c) if half == 0 else c, i)
                  wC2, wC2k = load_w(w_in[l].rearrange("(k p) n -> p k n", p=128)[:, :, 1536:1792], KD, 256)
                  for i in range(len(SUBT)):
                      o, n = cur(i)
                      for j in range(2):
                          pst, pk, _ = proj(wC2, wC2k, i, j)
                          zk_ = zkey(j, i)
                          tt('dve', Z[:, j, PAD + o:PAD + o + n], Z[:, j, PAD + o:PAD + o + n], pst[:, 0:n], ALU.mult, [zk_, pk], [zk_])
                  for i in range(len(SUBT)):
                      o, n = cur(i)
                      for j in range(2):
                          pc, pck = ps_next()
                          rd = [zkey(j, i), ('Zpad', j)] + ([zkey(j, i - 1)] if i > 0 else []) + [('DG3', q, j) for q in range(3)]
                          conv_diag(pc, n, pck, [DG3[:, q * 2 + j, :] for q in range(3)],
                                    (lambda j, o, n: lambda q: Z[:, j, PAD + o - 2 + q:PAD + o - 2 + q + n])(j, o, n), rd)
                          tt('dve', Y[:, 4 + j, o:o + n], Z[:, 2 + j, PAD + o:PAD + o + n], pc[:, 0:n], ALU.mult, [zkey(2 + j, i), pck], [('Y', 4 + j, i)])
                  chk('C', l, pq)
                  wD, wDk = load_w(w_in[l].rearrange("(k p) n -> p k n", p=128)[:, :, 1792:2048], KD, 256)
                  for i in range(len(SUBT)):
                      o, n = cur(i)
                      for j in range(2):
                          pst, pk, _ = proj(wD, wDk, i, j)
                          evac_z(pst, pk, n, j, i)
                  for i in range(len(SUBT)):
                      o, n = cur(i)
                      pls = []
                      for j in range(2):
                          ntap = 4 if j == 0 else 16
                          tb0 = 6 + (0 if j == 0 else 4)
                          pc, pck = ps_next()
                          rd = [zkey(j, i), ('Zpad', j), 'CSTb'] + ([zkey(j, i - 1)] if i > 0 else [])
                          conv_diag(pc, n, pck, [CSTb[:, tb0 + q, :] for q in range(ntap)],
                                    (lambda j, o, n: lambda q: Z[:, j, PAD + o - q:PAD + o - q + n])(j, o, n), rd)
                          plb, plbk = tb_next()
                          if pq == 0 and i == 0:
                              pl, plk = tmp_next()
                              cp('act', pl[:, 0:n], pc[:, 0:n], [pck], [plk])
                              tt('dve', pl[:, HALO - o:HALO - o + 16], pl[:, HALO - o:HALO - o + 16], CNT[:, j, :], ALU.mult, [plk, 'CNT'], [plk])
                              tt('dve', plb[:, 0:n], pl[:, 0:n], Z[:, j, PAD + o:PAD + o + n], ALU.subtract, [plk, zkey(j, i)], [plbk])
                          else:
                              tt('dve', plb[:, 0:n], pc[:, 0:n], Z[:, j, PAD + o:PAD + o + n], ALU.subtract, [pck, zkey(j, i)], [plbk])
                          pw, pwk = ps_next()
                          mm(pw[:, 0:n], [(POOLBD[:, j, :], plb[:, 0:n])], ['POOLBD', plbk], pwk)
                          act(Y[:, 6 + j, o:o + n], pw[:, 0:n], AF.Identity, [pwk, 'LNC'], [('Y', 6 + j, i)], scale=LNC[:, PSC + j:PSC + j + 1], bias=0.0)

                  chk('D', l, pq)
                  st0['v'] = 126 if l == 0 else 254
                  for dc in range(KD):
                      for kb in range(4):
                          c0 = 2048 + kb * 1024 + dc * 128
                          wg, wgk = load_w(w_in[l].rearrange("(k p) n -> p k n", p=128)[:, :, c0:c0 + 128], KD, 128)
                          wb_, wbk = load_w(w_branch[l, kb].rearrange("(j p) d -> p j d", p=128)[:, :, dc * 128:(dc + 1) * 128], 2, 128)
                          for i in range(len(SUBT)):
                              o, n = cur(i)
                              acc, acck = ACC[i], ('acc', i)
                              pg, pgk, _ = proj(wg, wgk, i, 0)
                              sg, sgk = tmp_next()
                              act(sg[:, 0:n], pg[:, 0:n], AF.Sigmoid, [pgk], [sgk])
                              pb, pbk = ps_next()
                              mm(pb[:, 0:n], [(wb_[:, j, :], Y[:, 2 * kb + j, o:o + n]) for j in range(2)],
                                 [wbk, ('Y', 2 * kb, i), ('Y', 2 * kb + 1, i)], pbk)
                              if kb == 0:
                                  tt('dve', acc[:, 0:n], pb[:, 0:n], sg[:, 0:n], ALU.mult, [pbk, sgk], [acck])
                              else:
                                  tt('dve', sg[:, 0:n], pb[:, 0:n], sg[:, 0:n], ALU.mult, [pbk, sgk], [sgk])
                                  if kb < 3:
                                      tt('dve', acc[:, 0:n], acc[:, 0:n], sg[:, 0:n], ALU.add, [acck, sgk], [acck])
                                  else:
                                      tt('dve', MG[:, dc, o:o + n], acc[:, 0:n], sg[:, 0:n], ALU.add, [acck, sgk], [('MG', dc, i)])
                  chk('merge', l, pq)
                  for dc in range(KD):
                      wo, wok = load_w(w_out[l].rearrange("(k p) n -> p k n", p=128)[:, :, dc * 128:(dc + 1) * 128], KD, 128)
                      for i in range(len(SUBT)):
                          o, n = cur(i)
                          pst, pk = ps_next()
                          mm(pst[:, 0:n], [(wo[:, k, :], MG[:, k, o:o + n]) for k in range(KD)], [wok] + [('MG', k, i) for k in range(KD)], pk)
                          tt('dve', X[:, dc, o:o + n], X[:, dc, o:o + n], pst[:, 0:n], ALU.add, [xk(dc, i), pk], [xk(dc, i)])

                  chk('xmid', l, pq)
                  rmsnorm((l * 3 + 1) * KD, 'H', True, pq)
                  for (j0, nj) in JGROUPS:
                      for jj in range(nj):
                          j = j0 + jj
                          wg, wgk = load_w(w_up[l].rearrange("(k p) n -> p k n", p=128)[:, :, j * 128:(j + 1) * 128], KD, 128)
                          wv, wvk = load_w(w_up[l].rearrange("(k p) n -> p k n", p=128)[:, :, DFF + j * 128:DFF + (j + 1) * 128], KD, 128)
                          for i in range(len(SUBT)):
                              o, n = cur(i)
                              res = []
                              for (U, un, cj, ww, wwk) in ((UG, 'UG', j, wg, wgk), (UV, 'UV', NJ + j, wv, wvk)):
                                  pst, pk, _ = proj(ww, wwk, i, 0)
                                  cp('act', U[:, PAD + o:PAD + o + n], pst[:, 0:n], [pk], [(un, i)])
                                  a, ak = tmp_next()
                                  wbase = l * 132 + cj
                                  rd = [(un, i), un + 'pad', 'FWC', 'FBC'] + ([(un, i - 1)] if i > 0 else [])
                                  if un == 'UG':
                                      act(a[:, 0:n], pst[:, 0:n], AF.Identity, [pk, 'FWC', 'FBC'], [ak],
                                          scale=FWC[:, wbase + 88:wbase + 89], bias=FBC[:, l * 44 + cj:l * 44 + cj + 1])
                                  else:
                                      ts('pool', a[:, 0:n], U[:, PAD + o:PAD + o + n], FWC[:, wbase + 88:wbase + 89], FBC[:, l * 44 + cj:l * 44 + cj + 1],
                                         ALU.mult, ALU.add, rd, [ak])
                                  stt('dve', a[:, 0:n], U[:, PAD + o - 1:PAD + o - 1 + n], FWC[:, wbase + 44:wbase + 45], a[:, 0:n], ALU.mult, ALU.add, rd + [ak], [ak])
                                  stt('dve', a[:, 0:n], U[:, PAD + o - 2:PAD + o - 2 + n], FWC[:, wbase:wbase + 1], a[:, 0:n], ALU.mult, ALU.add, rd + [ak], [ak])
                                  res.append((a, ak))
                              (ga, gak), (va, vak) = res
                              sg, sgk = tmp_next()
                              act(sg[:, 0:n], ga[:, 0:n], AF.Silu, [gak], [sgk])
                              tt('dve', Y[:, jj, o:o + n], sg[:, 0:n], va[:, 0:n], ALU.mult, [sgk, vak], [('Y', jj, i)])
                      for dc in range(KD):
                          wd, wdk = load_w(w_down[l].rearrange("(k p) n -> p k n", p=128)[:, j0:j0 + nj, dc * 128:(dc + 1) * 128], nj, 128)
                          for i in range(len(SUBT)):
                              o, n = cur(i)
                              pst, pk = ps_next()
                              mm(pst[:, 0:n], [(wd[:, jj, :], Y[:, jj, o:o + n]) for jj in range(nj)], [wdk] + [('Y', jj, i) for jj in range(nj)], pk)
                              tt('dve', X[:, dc, o:o + n], X[:, dc, o:o + n], pst[:, 0:n], ALU.add, [xk(dc, i), pk], [xk(dc, i)])

                  chk('ffn', l, pq)
                  rmsnorm((l * 3 + 2) * KD, 'H', False)
                  ptkeys = [zkey(jz, iz) for jz in range(2) for iz in range(len(SUBT))]
                  cdma(Z[:, 0:2, PAD:PAD + T], pT[pq, l].rearrange("(j p) t -> p j t", p=128), ptkeys, 'PT')
                  for dc in range(KD):
                      wg, wgk = load_w(w_pg[l].rearrange("(k p) n -> p k n", p=128)[:, :, dc * 128:(dc + 1) * 128], KD, 128)
                      wp, wpk = load_w(w_pp[l].rearrange("(j p) n -> p j n", p=128)[:, :, dc * 128:(dc + 1) * 128], 2, 128)
                      for i in range(len(SUBT)):
                          o, n = cur(i)
                          pg, pgk, _ = proj(wg, wgk, i, 0)
                          sg, sgk = tmp_next()
                          act(sg[:, 0:n], pg[:, 0:n], AF.Sigmoid, [pgk], [sgk])
                          pp, ppk = ps_next()
                          mm(pp[:, 0:n], [(wp[:, j, :], Z[:, j, PAD + o:PAD + o + n]) for j in range(2)], [wpk, zkey(0, i), zkey(1, i)], ppk)
                          tt('dve', sg[:, 0:n], pp[:, 0:n], sg[:, 0:n], ALU.mult, [ppk, sgk], [sgk])
                          tt('dve', X[:, dc, o:o + n], X[:, dc, o:o + n], sg[:, 0:n], ALU.add, [xk(dc, i), sgk], [xk(dc, i)])

              chk('end', L - 1, pq)
              st0['v'] = 0
              fin_sub = [(256, 256, 0), (512, 512, 1), (1024, 256, 2)]
              for (o, n, i) in fin_sub:
                  pst, pk = ps_next()
                  for k in range(KD):
                      tb, tk = tb_next()
                      act(tb[:, 0:n], X[:, k, o:o + n], AF.Square, [xk(k, i)], [tk])
                      S.add('pe', (lambda tb, k, pst, o, n: lambda e: e.matmul(pst[:, 0:n], lhsT=CSTb[:, ONES, :], rhs=tb[:, 0:n], start=(k == 0), stop=(k == KD - 1)))(tb, k, pst, o, n),
                            reads=[tk, 'CSTb'], writes=[pk])
                  r, rk = tmp_next()
                  rsqrt_into(r[:, 0:n], pst[:, 0:n], 1.0 / D, [pk], rk, n)
                  for k in range(KD):
                      ot, otk = tmp_next()
                      stt('dve', ot[:, 0:n], X[:, k, o:o + n], GC[:, L * 3 * KD + k:L * 3 * KD + k + 1], r[:, 0:n], ALU.mult, ALU.mult, [xk(k, i), rk, 'GC'], [otk])
                      okey = ('out', pq, k, o)
                      S.add('sp', (lambda ot, k, o, n, pq: lambda e: e.dma_start(out=outT[pq, k * 128:(k + 1) * 128, o - 256:o - 256 + n], in_=ot[:, 0:n]))(ot, k, o, n, pq),
                            reads=[otk], writes=[okey], dma_key=('o', k))
                      outs.append(S.ops[-1])

        except _Stop:
            pass
        if debug is not None:
            allkeys = list(S.last_writer.keys())
            for nm, (dap, tns) in dbg.items():
                S.add('sp', (lambda dap, tns: lambda e: e.dma_start(out=dap[:], in_=tns[:]))(dap, tns), reads=allkeys, writes=[('dbg', nm)], dma_key=('dbg', nm))
                outs.append(S.ops[-1])
        S.finalize(nc, st)
        with nc.Block() as block0:
            @block0.gpsimd
            def _(e):
                for sem in S.sems.values():
                    e.sem_clear(sem)
                for sem in S.sems.values():
                    e.wait_op(sem, 0, "sem-eq")
        with nc.Block() as block:
            @block.sync
            def _(e):
                fw = {}
                for op in outs:
                    fw[op.token[0]] = max(fw.get(op.token[0], 0), op.token[1])
                S.emit_engine('sp', e, list(fw.items()))

            @block.scalar
            def _(e): S.emit_engine('act', e)

            @block.vector
            def _(e): S.emit_engine('dve', e)

            @block.gpsimd
            def _(e): S.emit_engine('pool', e)

            @block.tensor
            def _(e): S.emit_engine('pe', e)
    return nc


def _cols(v):
    return np.ascontiguousarray(np.asarray(v, np.float32).reshape(-1, 128).T)


def prepare(x, p, g_mix, w_in, gmlp_ln_g, gmlp_ln_b, gmlp_w_s, gmlp_b_s,
           conf_w_dw, conf_b_dw, conf_ln_g, conf_ln_b, short_w, pool_w, pool_scale,
           w_branch, w_out, g_ffn, ffn_w_up, ffn_w_conv, ffn_b_conv, ffn_w_down,
           g_ple, ple_w_gate, ple_w_proj, g_final):
    f = lambda a: np.ascontiguousarray(np.asarray(a, np.float32))
    x = f(x); p = f(p)
    B, SEQ, _ = x.shape
    gcols = np.concatenate([_cols(v[l]) for l in range(L) for v in (g_mix, g_ffn, g_ple)] + [_cols(g_final)], axis=1)
    lncols = np.concatenate([_cols(np.asarray(v)[l].reshape(-1)) for l in range(L)
                             for v in (gmlp_ln_g, gmlp_ln_b, conf_b_dw, conf_ln_g, conf_ln_b, pool_scale)], axis=1)
    shortw = np.concatenate([_cols(np.asarray(short_w)[l, q]) for l in range(L) for q in range(3)], axis=1)
    confw = np.concatenate([_cols(np.asarray(conf_w_dw)[l, q]) for l in range(L) for q in range(31)], axis=1)
    ffnwc = np.concatenate([_cols(np.asarray(ffn_w_conv)[l, q]) for l in range(L) for q in range(3)], axis=1)
    ffnbc = np.concatenate([_cols(np.asarray(ffn_b_conv)[l]) for l in range(L)], axis=1)
    gwT = f(np.transpose(np.asarray(gmlp_w_s), (0, 1, 3, 2)))
    gbs = f(np.asarray(gmlp_b_s).reshape(1, -1))
    pw = np.asarray(pool_w, np.float32)
    poolbd = np.zeros((L, 2, 128, 128), np.float32)
    for l in range(L):
        for j in range(2):
            poolbd[l, j, :64, :64] = pw[l, 2 * j]
            poolbd[l, j, 64:, 64:] = pw[l, 2 * j + 1]
    consts = np.zeros((26, 128, 128), np.float32)
    consts[0] = np.eye(128)
    consts[1] = np.triu(np.ones((128, 128)))
    consts[2][:, :64] = 1.0
    consts[3][:, 64:] = 1.0
    consts[4][:64, :64] = 1.0 / 64; consts[4][64:, 64:] = 1.0 / 64
    consts[5] = 1.0
    pidx = np.arange(128)
    for j in range(2):
        ntap = 4 if j == 0 else 16
        tb0 = 6 + (0 if j == 0 else 4)
        win = np.where(pidx < 64, WINS[2 * j], WINS[2 * j + 1]).astype(np.float32)
        for q in range(ntap):
            consts[tb0 + q][pidx, pidx] = np.where(q < win, 1.0 / win, 0.0)
    in_maps = []
    for c in range(8):
        b, seg = c // 4, c % 4
        xT = np.zeros((NPASS, D, T), np.float32)
        pT = np.zeros((NPASS, L, 256, T), np.float32)
        mask = np.full((128, HALO), 0.0 if seg == 0 else 1.0, np.float32)
        cnt = np.ones((128, 2, 16), np.float32)
        tpos16 = seg * 2048 + np.arange(16)
        for j in range(2):
            win = np.where(pidx < 64, WINS[2 * j], WINS[2 * j + 1]).astype(np.float32)[:, None]
            cnt[:, j, :] = win / np.minimum(tpos16[None, :] + 1, win)
        for q in range(NPASS):
            s = seg * 2048 + q * TOUT - HALO
            lo = max(s, 0)
            xT[q, :, lo - s:] = x[b, lo:s + T].T
            for l in range(L):
                pT[q, l, :, lo - s:] = p[l, b, lo:s + T].T
        in_maps.append({
            "xT": xT, "pT": pT, "mask": mask, "cnt": cnt,
            "w_in": f(w_in), "w_branch": f(w_branch), "w_out": f(w_out), "w_up": f(ffn_w_up), "w_down": f(ffn_w_down),
            "w_pg": f(ple_w_gate), "w_pp": f(ple_w_proj), "gcols": f(gcols), "lncols": f(lncols), "shortw": f(shortw),
            "confw": f(confw), "ffnwc": f(ffnwc), "ffnbc": f(ffnbc), "gwT": gwT, "gbs": gbs, "poolbd": poolbd, "consts": consts,
        })
    return in_maps


def kernel(**inputs):
    in_maps = prepare(**inputs)
    B, SEQ = 2, 8192
    nc = build_nc()
    res = run_bass_kernel_spmd(nc, in_maps, core_ids=list(range(8)))
    out = np.zeros((B, SEQ, D), np.float32)
    for c in range(8):
        b, seg = c // 4, c % 4
        o = res.results[c]["outT"]
        for q in range(NPASS):
            s = seg * 2048 + q * TOUT
            out[b, s:s + TOUT, :] = o[q].T
    return out
```
